# Optimizing a Trainium2 kernel written in Bass

```python
import jax, jax.numpy as jnp
from jax import lax
import numpy as np

D_MODEL = 1024
BATCH = 1
SEQ = 16384
DEPTH = 1
DEC_BATCH = 16
DEC_SEQ = 16
PAST_LEN = 2048

CHUNK = 64
GLA_HEADS = 4
GLA_DK = 128
GLA_DV = 256
GLA_KEY = GLA_HEADS * GLA_DK
GLA_VAL = GLA_HEADS * GLA_DV
GLA_LOWRANK = 16
GLA_TAU = 16.0
GMLP_CHUNK = 128
GMLP_GROUPS = 4
GMLP_WIDTH = 1024
GMLP_DG = GMLP_WIDTH // GMLP_GROUPS
N_BRANCH = 2
EPS = 1e-6
PROJ_SIZES = (GLA_KEY, GLA_KEY, GLA_VAL, GLA_VAL, GLA_LOWRANK,
              GMLP_WIDTH, GMLP_WIDTH, GMLP_WIDTH, D_MODEL, D_MODEL)
PROJ_COLS = GLA_KEY * 2 + GLA_VAL * 2 + GLA_LOWRANK + GMLP_WIDTH * 3 + D_MODEL * N_BRANCH

kernel_name = "hybrid_gla_gmlp_stream_step"


def _split_points():
    pts, acc = [], 0
    for s in PROJ_SIZES[:-1]:
        acc += s
        pts.append(acc)
    return pts


def rmsnorm(x, w):
    xf = x.astype(jnp.float32)
    y = xf * lax.rsqrt(jnp.mean(xf * xf, axis=-1, keepdims=True) + EPS)
    return (y * w.astype(jnp.float32)).astype(x.dtype)


def layernorm(x, w, b):
    xf = x.astype(jnp.float32)
    mu = jnp.mean(xf, axis=-1, keepdims=True)
    var = jnp.mean(jnp.square(xf - mu), axis=-1, keepdims=True)
    y = (xf - mu) * lax.rsqrt(var + EPS)
    return (y * w.astype(jnp.float32) + b.astype(jnp.float32)).astype(x.dtype)


def gla_chunked(q, k, v, log_a, S0):
    B, L, H, _ = q.shape
    C = min(CHUNK, L)
    N = L // C
    def to_chunks(t):
        return t.astype(jnp.float32).reshape(B, N, C, H, t.shape[-1]).transpose(1, 0, 2, 3, 4)
    qc, kc, vc = to_chunks(q), to_chunks(k), to_chunks(v)
    bc = jnp.cumsum(to_chunks(log_a), axis=2)
    mask = jnp.tril(jnp.ones((C, C), dtype=bool))

    def step(S, inp):
        qn, kn, vn, bn = inp
        diff = bn[:, :, None] - bn[:, None, :]
        dec = jnp.exp(jnp.where(mask[None, :, :, None, None], diff, -jnp.inf))
        A = jnp.einsum('bthk,btshk,bshk->bhts', qn, dec, kn)
        intra = jnp.einsum('bhts,bshv->bthv', A, vn)
        inter = jnp.einsum('bthk,bhkv->bthv', qn * jnp.exp(bn), S)
        b_last = bn[:, -1]
        S = jnp.exp(b_last)[..., None] * S + jnp.einsum(
            'bshk,bshv->bhkv', kn * jnp.exp(b_last[:, None] - bn), vn)
        return S, intra + inter

    S_fin, o = lax.scan(step, S0.astype(jnp.float32), (qc, kc, vc, bc))
    o = o.transpose(1, 0, 2, 3, 4).reshape(B, L, H, v.shape[-1])
    return o, S_fin


def mixer_layer(x, c, S0, norm_w, w_ada, b_ada, w_in, w_a2, b_a, gla_norm_w,
                ln_v_w, ln_v_b, w_s, b_s, b_gate, w_proj_a, w_proj_b, w_out):
    B, L, _ = x.shape
    ada = jax.nn.silu(c) @ w_ada + b_ada
    shift, scale, gate = jnp.split(ada[:, None, :], 3, axis=-1)
    h = rmsnorm(x, norm_w) * (1.0 + scale) + shift
    proj = h @ w_in
    q, k, v, r, a_lr, u, gv, z, g_a, g_b = jnp.split(proj, _split_points(), axis=-1)

    log_a = jax.nn.log_sigmoid((a_lr @ w_a2 + b_a).astype(jnp.float32)) / GLA_TAU
    q = q.reshape(B, L, GLA_HEADS, GLA_DK) * (GLA_DK ** -0.5)
    k = k.reshape(B, L, GLA_HEADS, GLA_DK)
    v = v.reshape(B, L, GLA_HEADS, GLA_DV)
    log_a = log_a.reshape(B, L, GLA_HEADS, GLA_DK)
    o, S_new = gla_chunked(q, k, v, log_a, S0)
    o = rmsnorm(o, gla_norm_w).astype(x.dtype).reshape(B, L, GLA_VAL)
    y_a = (o * jax.nn.silu(r)) @ w_proj_a

    vn = layernorm(gv, ln_v_w, ln_v_b)
    C = min(GMLP_CHUNK, L)
    N = L // C
    idx = jnp.arange(C) // CHUNK
    smask = idx[:, None] >= idx[None, :]
    ws = jnp.where(smask[None], w_s[:, :C, :C], 0.0)
    vg = vn.reshape(B, N, C, GMLP_GROUPS, GMLP_DG)
    s = jnp.einsum('gij,bnjgc->bnigc', ws, vg) + b_s[:, :C].T[None, None, :, :, None]
    s = s.reshape(B, L, GMLP_WIDTH)
    y_b = (u * s * jax.nn.silu(z)) @ w_proj_b

    merged = jax.nn.sigmoid(g_a + b_gate[0]) * y_a + jax.nn.sigmoid(g_b + b_gate[1]) * y_b
    x = x + gate * (merged @ w_out)
    return x, S_new, vn


def setup_inputs(seed: int = 0) -> dict:
    key = jax.random.key(seed)
    ks = jax.random.split(key, 24)
    f32 = jnp.float32
    nrm = lambda k, shape, s: jax.random.normal(k, shape, f32) * s
    return {
        'x_prompt': nrm(ks[0], (BATCH, SEQ, D_MODEL), 1.0),
        'x_sample': nrm(ks[1], (DEC_BATCH, DEC_SEQ, D_MODEL), 1.0),
        'state_gla': nrm(ks[2], (DEPTH, DEC_BATCH, GLA_HEADS, GLA_DK, GLA_DV), 0.1),
        'c_prompt': nrm(ks[3], (BATCH, D_MODEL), 1.0),
        'c_sample': nrm(ks[4], (DEC_BATCH, D_MODEL), 1.0),
        'norm_w': 1.0 + nrm(ks[5], (DEPTH, D_MODEL), 0.02),
        'w_ada': nrm(ks[6], (DEPTH, D_MODEL, 3 * D_MODEL), 0.5 * D_MODEL ** -0.5),
        'b_ada': nrm(ks[7], (DEPTH, 3 * D_MODEL), 0.02),
        'w_in': nrm(ks[8], (DEPTH, D_MODEL, PROJ_COLS), D_MODEL ** -0.5),
        'w_a2': nrm(ks[9], (DEPTH, GLA_LOWRANK, GLA_KEY), GLA_LOWRANK ** -0.5),
        'b_a': nrm(ks[10], (DEPTH, GLA_KEY), 0.1),
        'gla_norm_w': 1.0 + nrm(ks[11], (DEPTH, GLA_DV), 0.02),
        'ln_v_w': 1.0 + nrm(ks[12], (DEPTH, GMLP_WIDTH), 0.02),
        'ln_v_b': nrm(ks[13], (DEPTH, GMLP_WIDTH), 0.02),
        'w_s': nrm(ks[14], (DEPTH, GMLP_GROUPS, GMLP_CHUNK, GMLP_CHUNK), GMLP_CHUNK ** -0.5),
        'b_s': 1.0 + nrm(ks[15], (DEPTH, GMLP_GROUPS, GMLP_CHUNK), 0.02),
        'b_gate': nrm(ks[16], (DEPTH, N_BRANCH, D_MODEL), 0.02),
        'w_proj_a': nrm(ks[17], (DEPTH, GLA_VAL, D_MODEL), GLA_VAL ** -0.5),
        'w_proj_b': nrm(ks[18], (DEPTH, GMLP_WIDTH, D_MODEL), GMLP_WIDTH ** -0.5),
        'w_out': nrm(ks[19], (DEPTH, D_MODEL, D_MODEL), D_MODEL ** -0.5),
        'final_norm_w': 1.0 + nrm(ks[20], (D_MODEL,), 0.02),
    }


def reference(x_prompt, x_sample, state_gla, c_prompt, c_sample, norm_w, w_ada, b_ada,
              w_in, w_a2, b_a, gla_norm_w, ln_v_w, ln_v_b, w_s, b_s, b_gate,
              w_proj_a, w_proj_b, w_out, final_norm_w):
    xp, xs = x_prompt, x_sample
    sp_list, ss_list, vs_list = [], [], []
    for l in range(DEPTH):
        params = (norm_w[l], w_ada[l], b_ada[l], w_in[l], w_a2[l], b_a[l], gla_norm_w[l],
                  ln_v_w[l], ln_v_b[l], w_s[l], b_s[l], b_gate[l],
                  w_proj_a[l], w_proj_b[l], w_out[l])
        s0_prompt = jnp.zeros((xp.shape[0], GLA_HEADS, GLA_DK, GLA_DV), jnp.float32)
        xp, sp, _ = mixer_layer(xp, c_prompt, s0_prompt, *params)
        xs, ss, vs = mixer_layer(xs, c_sample, state_gla[l], *params)
        sp_list.append(sp)
        ss_list.append(ss)
        vs_list.append(vs)
    y_prompt = rmsnorm(xp, final_norm_w)
    y_sample = rmsnorm(xs, final_norm_w)
    new_state_gla_prompt = jnp.stack(sp_list)
    new_state_gla_sample = jnp.stack(ss_list)
    new_gmlp_v_sample = jnp.stack(vs_list)
    return (y_prompt, y_sample, new_state_gla_prompt, new_state_gla_sample, new_gmlp_v_sample)
```

```python
import os
import numpy as np
from contextlib import ExitStack
import concourse.bass as bass
import concourse.mybir as mybir
from concourse.bass_utils import run_bass_kernel_spmd

F32 = mybir.dt.float32
BF16 = mybir.dt.bfloat16
AF = mybir.ActivationFunctionType
ALU = mybir.AluOpType

NCORES = 8
D = 1024
TOK = 2048
NTILE = 4
EPS = 1e-6
C_Q, C_K, C_V, C_R, C_A, C_U, C_GV, C_Z, C_GA, C_GB = 0, 512, 1024, 2048, 3072, 3088, 4112, 5136, 6160, 7184
PROJ_COLS = 8208


class Sched:
    def __init__(self, nc, same_engine_sync=("act", "dve", "pool")):
        self.nc = nc
        self.eng = {"pe": nc.tensor, "act": nc.scalar, "dve": nc.vector,
                    "pool": nc.gpsimd, "sp": nc.sync}
        self.streams = {e: [] for e in self.eng}
        self.count = {}
        self.known = {e: {} for e in self.eng}
        self.last_write = {}
        self.readers = {}
        self.same_engine_sync = set(same_engine_sync)
        self.semkeys = []

    def op(self, engine, fn, reads=(), writes=(), dma=None, ndma=1):
        if getattr(self, "stopped", False):
            return ("pe", 0)

        def _expand(keys):
            out = []
            for k in keys:
                if k.startswith("wbuf") and "_" not in k:
                    out += [f"{k}_{q}" for q in range(4)]
                else:
                    out.append(k)
            return out
        reads, writes = _expand(reads), _expand(writes)
        deps = {}

        def add(ev):
            if ev is not None and deps.get(ev[0], 0) < ev[1]:
                deps[ev[0]] = ev[1]

        for k in reads:
            add(self.last_write.get(k))
        for k in writes:
            add(self.last_write.get(k))
            for r in self.readers.get(k, ()):
                add(r)
        waits = []
        kn = self.known[engine]
        for k, v in deps.items():
            if dma is None and k == engine and engine not in self.same_engine_sync:
                continue
            if kn.get(k, 0) >= v:
                continue
            kn[k] = v
            waits.append((k, v))
        semkey, amt = (engine, 1) if dma is None else (dma, 16 * ndma)
        if ndma == -1:
            amt = 1
        if semkey not in self.count:
            self.count[semkey] = 0
            self.semkeys.append(semkey)
        self.count[semkey] += amt
        ev = (semkey, self.count[semkey])
        self.streams[engine].append((waits, fn, semkey, (ndma if dma is not None else 0)))
        for k in reads:
            self.readers.setdefault(k, []).append(ev)
        for k in writes:
            self.last_write[k] = ev
            self.readers[k] = []
        return ev

    def barrier(self):
        evs = [(k, self.count[k]) for k in self.semkeys if self.count[k] > 0]
        for e in self.eng:
            waits = []
            for k, v in evs:
                if self.known[e].get(k, 0) < v:
                    self.known[e][k] = v
                    waits.append((k, v))
            self.streams[e].append((waits, None, None, 0))

    def final_waits(self, engine, events):
        best = {}
        for k, v in events:
            best[k] = max(best.get(k, 0), v)
        self.streams[engine].append((list(best.items()), None, None, 0))

    def run(self, engine, e, sems):
        for waits, fn, semkey, ndma in self.streams[engine]:
            for k, v in waits:
                e.wait_ge(sems[k], v)
            if fn is None:
                continue
            r = fn(e)
            if ndma == -1:
                r.then_inc(sems[semkey], 1)
            elif ndma:
                if not isinstance(r, (list, tuple)):
                    r = [r]
                assert len(r) == ndma, (len(r), ndma)
                for ins in r:
                    ins.then_inc(sems[semkey], 16)
            else:
                r.then_inc(sems[semkey], 1)


class Bank:
    def __init__(self, tiles):
        self.tiles = tiles
        self.free_list = list(range(len(tiles)))

    def alloc(self):
        assert self.free_list, "out of PSUM banks"
        return self.free_list.pop(0)

    def free(self, i):
        self.free_list.append(i)


class Grp:
    pass


def build_nc():
    nc = bass.Bass("TRN2", target_bir_lowering=False)

    def din(name, shape):
        return nc.dram_tensor(name, shape, F32, kind="ExternalInput").ap()

    def dout(name, shape):
        return nc.dram_tensor(name, shape, F32, kind="ExternalOutput").ap()

    x_d = din("x", [TOK, D])
    xs_d = din("xs", [32, D])
    s0_d = din("s0", [2, 4, 128, 256])
    c3_d = din("c3", [3, D])
    norm_w_d = din("norm_w", [1, D])
    w_ada_d = din("w_ada", [D, 3 * D])
    b_ada_d = din("b_ada", [1, 3 * D])
    w_in_d = din("w_in", [D, PROJ_COLS])
    w_a2_d = din("w_a2", [16, 512])
    b_a_d = din("b_a", [1, 512])
    gnw_d = din("gnw", [1, 256])
    lnw_h = nc.dram_tensor("lnw", [1, D], F32, kind="ExternalInput")
    lnb_h = nc.dram_tensor("lnb", [1, D], F32, kind="ExternalInput")
    lnw_d, lnb_d = lnw_h.ap(), lnb_h.ap()
    w_s_d = din("w_s", [4, 128, 128])
    b_s_d = din("b_s", [4, 128])
    b_gate_d = din("b_gate", [2, D])
    wpa_d = din("wpa", [D, D])
    wpb_d = din("wpb", [D, D])
    wo_d = din("wo", [D, D])
    fnw_h = nc.dram_tensor("fnw", [1, D], F32, kind="ExternalInput")
    fnw_d = fnw_h.ap()
    NPREV = 28
    xprev_d = din("xprev", [NPREV * 512, D])
    pmask_d = din("pmask", [128, NPREV])
    ident_d = din("ident", [128, 128])
    masku_d = din("masku", [128, 128])
    maskus_d = din("maskus", [32, 32])
    wsmask_d = din("wsmask", [128, 128])

    y_d = dout("y", [TOK, D])
    ys_d = dout("ys", [32, D])
    sp_d = dout("sp_out", [128, 4, 256])
    ss_d = dout("ss_out", [2, 128, 4, 256])
    vn_d = dout("vn_out", [32, D])

    ag_in = nc.dram_tensor("ag_in", [128, 1028], F32)
    ag_out = nc.dram_tensor("ag_out", [NCORES * 128, 1028], F32)

    S = Sched(nc)
    es = ExitStack()
    with es:
        def sb(name, shape, dt=F32):
            return es.enter_context(nc.sbuf_tensor("s_" + name, shape, dt))

        psum_tiles = [es.enter_context(nc.psum_tensor(f"pb{i}", [128, 512], F32)) for i in range(8)]
        PB = Bank(psum_tiles)

        def pkey(i):
            return f"pb{i}"

        identf = sb("identf", [128, 128])
        identb = sb("identb", [128, 128], BF16)
        masku = sb("masku", [128, 128])
        maskub = sb("maskub", [128, 128], BF16)
        maskus = sb("maskus", [32, 32])
        maskusb = sb("maskusb", [32, 32], BF16)
        wsmask = sb("wsmask", [128, 128])
        pmaskt = sb("pmaskt", [128, 28])
        onesf = sb("onesf", [128, 128])
        Rrows = sb("Rrows", [82, 128])
        RT = sb("RT", [128, 82])
        scT = sb("scT", [128, 3, 8], BF16)
        adaT = sb("adaT", [128, 24, 3])
        modA = sb("modA", [128, 3, 8])
        hbg = sb("hbg", [128, 2, 8])
        shareA = sb("shareA", [128, 2 * D])
        g05p = shareA[:, 0:D]
        fnw_bc = shareA[:, D:2 * D]
        g05s = sb("g05s", [32, D])
        wa = sb("wa", [128, 8, 16], BF16)
        wa2f = sb("wa2f", [17, 512])
        wa2 = sb("wa2", [17, 512], BF16)
        wsf = sb("wsf", [128, 128])
        WsT = sb("WsT", [128, 4, 128], BF16)
        WsTf = sb("WsTf", [128, 4, 128])
        wsfs = sb("wsfs", [32, 32])
        WsTs = sb("WsTs", [32, 4, 32], BF16)
        WsTfs = sb("WsTfs", [32, 4, 32])
        e2col = sb("e2col", [128, 2])
        bs2 = sb("bs2", [2, 4, 128])
        bs2s = sb("bs2s", [2, 4, 32])
        R2 = sb("R2", [2, 4, 128])
        R2s = sb("R2s", [2, 4, 32])
        Rb = sb("Rb", [128, 8, 128])
        Rbs = sb("Rbs", [128, 8, 32])

        NW = 3
        wbuf = [sb(f"wbuf{i}", [128, 8, 512], BF16) for i in range(NW)]

        class Ring:
            def __init__(self, name, shape, dt, nbuf):
                self.t = [sb(f"{name}{i}", shape, dt) for i in range(nbuf)]
                self.k = [f"{name}{i}" for i in range(nbuf)]
                self.i = 0

            def get(self):
                j = self.i % len(self.t)
                self.i += 1
                return self.t[j], self.k[j]

        x_ring = Ring("xld", [128, D], F32, 2)
        wst_ring = Ring("wst", [128, 2, 512], F32, 4)
        xr_ring = x_ring
        xo_ring = Ring("xo", [128, D], F32, 2)
        b1024 = Ring("b1k", [128, D], BF16, 3)
        xn_ring = b1024
        sr_ring = b1024
        za_ring = b1024
        st_ring = Ring("stat", [128, 8], F32, 6)
        f512 = Ring("f512", [128, 512], F32, 4)
        e_ring = f512
        tg_ring = f512
        tm_ring = f512
        sz_ring = f512
        s1_ring = f512
        sp_ring = Ring("spb", [128, 512], BF16, 2)
        khT_ring = Ring("khT", [128, 4, 128], BF16, 2)
        qm_ring = Ring("qm", [128, 4, 32], BF16, 2)
        ktok_ring = Ring("ktok", [128, 512], BF16, 2)
        atm_ring = Ring("atm", [128, 512], BF16, 2)

        GLt, GLk = xo_ring.t[0], xo_ring.k[0]
        GL = GLt[:].rearrange("p (j t) -> p j t", t=128)
        L2t, L2k = xo_ring.t[1], xo_ring.k[1]
        L2 = L2t[0:2, :]

        def sp_load(out_ap, in_ap, key, sem):
            return S.op("sp", lambda e: e.dma_start(out=out_ap, in_=in_ap), writes=[key], dma=sem)

        KSTOP = os.environ.get("KSTOP", "")

        def stop_here(tag):
            if KSTOP == tag:
                S.stopped = True

        sp_load(identf[:], ident_d, "identf", "d_c0")
        sp_load(masku[:], masku_d, "masku", "d_c1")
        sp_load(maskus[:], maskus_d, "maskus", "d_c2")
        sp_load(wsmask[:], wsmask_d, "wsmask", "d_c3")
        sp_load(pmaskt[:], pmask_d, "pmaskt", "d_c4")
        S.op("dve", lambda e: e.tensor_copy(out=identb[:], in_=identf[:]), reads=["identf"], writes=["identb"])
        S.op("dve", lambda e: e.tensor_copy(out=maskub[:], in_=masku[:]), reads=["masku"], writes=["maskub"])
        S.op("dve", lambda e: e.tensor_copy(out=maskusb[:], in_=maskus[:]), reads=["maskus"], writes=["maskusb"])
        S.op("dve", lambda e: e.memset(onesf[:], 1.0), writes=["onesf"])

        stop_here("c0")
        def rows_dma(e):
            r = []
            r.append(e.dma_start(out=Rrows[0:8, :], in_=norm_w_d.rearrange("o (k p) -> (o k) p", p=128)))
            r.append(e.dma_start(out=Rrows[8:10, :], in_=gnw_d.rearrange("o (k p) -> (o k) p", p=128)))
            r.append(e.dma_start(out=Rrows[10:18, :], in_=lnw_d.rearrange("o (k p) -> (o k) p", p=128)))
            r.append(e.dma_start(out=Rrows[18:34, :], in_=b_gate_d.rearrange("o (k p) -> (o k) p", p=128)))
            r.append(e.dma_start(out=Rrows[34:58, :], in_=b_ada_d.rearrange("o (k p) -> (o k) p", p=128)))
            r.append(e.dma_start(out=Rrows[58:82, :], in_=c3_d.rearrange("o (k p) -> (o k) p", p=128)))
            return r
        S.op("sp", rows_dma, writes=["Rrows"], dma="d_rows", ndma=6)
        b = PB.alloc()
        S.op("pe", lambda e, b=b: e.transpose(psum_tiles[b][:, 0:82], Rrows[:, :], identf[0:82, 0:82]),
             reads=["Rrows", "identf"], writes=[pkey(b)])
        S.op("dve", lambda e, b=b: e.tensor_copy(out=RT[:], in_=psum_tiles[b][:, 0:82]), reads=[pkey(b)], writes=["RT"])
        PB.free(b)
        S.op("act", lambda e: e.activation(out=scT[:].rearrange("p m k -> p (m k)"), in_=RT[:, 58:82], func=AF.Silu),
             reads=["RT"], writes=["scT"])
        S.op("dve", lambda e: e.tensor_scalar(out=hbg[:].rearrange("p m k -> p (m k)"), in0=RT[:, 18:34], scalar1=0.5,
                                              scalar2=None, op0=ALU.mult), reads=["RT"], writes=["hbg"])

        stop_here("c1")
        wq = []
        wstate = {"next": 0}

        def wload(i):
            src = wq[i]
            slot = i % NW
            srcv = src.rearrange("(k p) c -> p k c", p=128)
            for q in range(4):
                st, stk = wst_ring.get()
                S.op("sp", lambda e, st=st, q=q: e.dma_start(out=st[:], in_=srcv[:, 2 * q:2 * q + 2, :]),
                     writes=[stk], dma="d_" + stk)
                dst = wbuf[slot][:, 2 * q:2 * q + 2, :]
                wk = f"wbuf{slot}_{q}"
                if q == 0:
                    S.op("pool", lambda e, st=st, dst=dst: e.tensor_copy(out=dst, in_=st[:]), reads=[stk], writes=[wk])
                elif q == 2:
                    S.op("dve", lambda e, st=st, dst=dst: e.tensor_copy(out=dst, in_=st[:]), reads=[stk], writes=[wk])
                else:
                    S.op("act", lambda e, st=st, dst=dst: e.activation(out=dst, in_=st[:], func=AF.Copy), reads=[stk],
                         writes=[wk])

        released = set()

        def wneed(i):
            while wstate["next"] < len(wq):
                j = wstate["next"]
                if j - NW >= 0 and (j - NW) not in released:
                    assert j > i, ("weight slot still live", j, i)
                    break
                if j > i + NW - 1:
                    break
                wload(j)
                wstate["next"] += 1
            return i % NW

        def w_done(i):
            released.add(i)

        def colgrp(t, c0):
            return t[:, c0:c0 + 512]

        seq = []
        for j in range(6):
            seq.append(("ada", j, colgrp(w_ada_d, j * 512)))
        P1 = [("k", 0, colgrp(w_in_d, C_K)), ("v", 0, colgrp(w_in_d, C_V)), ("v", 1, colgrp(w_in_d, C_V + 512))]
        P2 = [("q", 0, colgrp(w_in_d, C_Q)), ("k", 0, colgrp(w_in_d, C_K)),
              ("v", 0, colgrp(w_in_d, C_V)), ("v", 1, colgrp(w_in_d, C_V + 512)),
              ("r", 0, colgrp(w_in_d, C_R)), ("r", 1, colgrp(w_in_d, C_R + 512)),
              ("pa", 0, colgrp(wpa_d, 0)), ("ga", 0, colgrp(w_in_d, C_GA)),
              ("pa", 1, colgrp(wpa_d, 512)), ("ga", 1, colgrp(w_in_d, C_GA + 512)),
              ("gv", 0, colgrp(w_in_d, C_GV)), ("gv", 1, colgrp(w_in_d, C_GV + 512)),
              ("u", 0, colgrp(w_in_d, C_U)), ("z", 0, colgrp(w_in_d, C_Z)),
              ("u", 1, colgrp(w_in_d, C_U + 512)), ("z", 1, colgrp(w_in_d, C_Z + 512)),
              ("pb", 0, colgrp(wpb_d, 0)), ("gb", 0, colgrp(w_in_d, C_GB)),
              ("pb", 1, colgrp(wpb_d, 512)), ("gb", 1, colgrp(w_in_d, C_GB + 512)),
              ("wo", 0, colgrp(wo_d, 0)), ("wo", 1, colgrp(wo_d, 512))]
        seq += [("p1",) + g for g in P1]
        for t in range(NTILE):
            seq += [("p2",) + g for g in P2]
        for s_ in seq:
            wq.append(s_[-1])
        wpos = {"i": 0}

        def next_w(expect):
            i = wpos["i"]
            assert seq[i][-3] == expect[0] and seq[i][-2] == expect[1], (seq[i][:-1], expect)
            slot = wneed(i)
            wpos["i"] += 1
            wlive.append(i)
            return slot

        wlive = []

        def w_release_all():
            for i in wlive:
                w_done(i)
            wlive.clear()

        waf = sb("waf", [128, 8, 16])
        S.op("sp", lambda e: e.dma_start(out=waf[:], in_=w_in_d[:, C_A:C_A + 16].rearrange("(k p) c -> p k c", p=128)),
             writes=["waf"], dma="d_wa")
        S.op("dve", lambda e: e.tensor_copy(out=wa[:], in_=waf[:]), reads=["waf"], writes=["wa"])

        def wa2_dma(e):
            return [e.dma_start(out=wa2f[0:16, :], in_=w_a2_d), e.dma_start(out=wa2f[16:17, :], in_=b_a_d)]
        S.op("sp", wa2_dma, writes=["wa2f"], dma="d_c5", ndma=2)
        S.op("dve", lambda e: e.tensor_copy(out=wa2[:], in_=wa2f[:]), reads=["wa2f"], writes=["wa2"])

        stop_here("c2")
        for j in range(6):
            slot = next_w(("ada", j))
            b = PB.alloc()

            def mm(e, slot=slot, b=b):
                r = None
                for fb in range(4):
                    for kc in range(8):
                        r = e.matmul(psum_tiles[b][:, fb * 3:fb * 3 + 3], lhsT=wbuf[slot][:, kc, fb * 128:(fb + 1) * 128],
                                     rhs=scT[:, :, kc], start=(kc == 0), stop=(kc == 7))
                return r
            S.op("pe", mm, reads=[f"wbuf{slot}", "scT"], writes=[pkey(b)])
            for fb in range(4):
                jj = j * 4 + fb
                S.op("dve", lambda e, b=b, fb=fb, jj=jj: e.tensor_scalar(
                    out=adaT[:, jj, :], in0=psum_tiles[b][:, fb * 3:fb * 3 + 3], scalar1=RT[:, 34 + jj:35 + jj],
                    scalar2=None, op0=ALU.add), reads=[pkey(b), "RT"], writes=["adaT"])
            PB.free(b)
            w_release_all()
        for m in range(3):
            S.op("dve", lambda e, m=m: e.tensor_scalar(out=modA[:, m, :], in0=adaT[:, 8:16, m], scalar1=1.0, scalar2=None,
                                                       op0=ALU.add), reads=["adaT"], writes=["modA"])
            S.op("dve", lambda e, m=m: e.tensor_tensor(out=modA[:, m, :], in0=modA[:, m, :], in1=RT[:, 0:8], op=ALU.mult),
                 reads=["modA", "RT"], writes=["modA"])

        stop_here("c3")
        def gate_bc(dst, dkey, n, cols_m):
            for j in range(8):
                for (c0, ncol, m) in cols_m:
                    S.op("dve", lambda e, j=j, c0=c0, ncol=ncol, m=m: e.tensor_scalar(
                        out=GL[:, j, c0:c0 + ncol], in0=onesf[:, c0:c0 + ncol], scalar1=adaT[:, 16 + j, m:m + 1],
                        scalar2=0.5, op0=ALU.mult, op1=ALU.mult), reads=["onesf", "adaT"], writes=[GLk])
            for half in range(2):
                b = PB.alloc()

                def mm(e, b=b, half=half):
                    r = None
                    for jj in range(4):
                        j = half * 4 + jj
                        r = e.matmul(psum_tiles[b][0:n, jj * 128:(jj + 1) * 128], lhsT=GL[:, j, 0:n], rhs=identf[:],
                                     start=True, stop=True)
                    return r
                S.op("pe", mm, reads=[GLk, "identf"], writes=[pkey(b)])
                S.op("act", lambda e, b=b, half=half: e.activation(out=dst[0:n, half * 512:(half + 1) * 512],
                                                                   in_=psum_tiles[b][0:n, :], func=AF.Copy),
                     reads=[pkey(b)], writes=[dkey])
                PB.free(b)
        gate_bc(g05s, "g05s", 32, [(0, 16, 1), (16, 16, 2)])


        stop_here("c4")
        S.op("dve", lambda e: e.memset(e2col[:], 0.0), writes=["e2col"])
        S.op("dve", lambda e: e.memset(e2col[:, 0:1], 1.0), reads=["e2col"], writes=["e2col"])
        S.op("dve", lambda e: e.memset(bs2[:], 0.0), writes=["bs2"])
        S.op("dve", lambda e: e.memset(bs2s[:], 0.0), writes=["bs2s"])
        S.op("dve", lambda e: e.memset(L2, 1.0), writes=[L2k])
        S.op("sp", lambda e: e.dma_start(out=L2t[0:1, :], in_=lnb_d), reads=[L2k], writes=[L2k], dma="d_c9")
        S.op("sp", lambda e: e.dma_start(out=bs2[1:2, :, :], in_=b_s_d.rearrange("(o g) i -> o g i", o=1)),
             reads=["bs2"], writes=["bs2"], dma="d_c10")

        def bs2s_dma(e):
            src = b_s_d[:, 0:16].rearrange("(o g) i -> o g i", o=1)
            return [e.dma_start(out=bs2s[1:2, :, 0:16], in_=src), e.dma_start(out=bs2s[1:2, :, 16:32], in_=src)]
        S.op("sp", bs2s_dma, reads=["bs2s"], writes=["bs2s"], dma="d_c11", ndma=2)

        stop_here("d1")
        for g in range(4):
            S.op("sp", lambda e, g=g: e.dma_start(out=wsf[:], in_=w_s_d[g]), writes=["wsf"], dma="d_c12")
            S.op("dve", lambda e: e.tensor_tensor(out=wsf[:], in0=wsf[:], in1=wsmask[:], op=ALU.mult),
                 reads=["wsf", "wsmask"], writes=["wsf"])
            if g == 0:
                stop_here("e0")
            b = PB.alloc()
            S.op("pe", lambda e, b=b: e.transpose(psum_tiles[b][:, 0:128], wsf[:], identf[:]), reads=["wsf", "identf"],
                 writes=[pkey(b)])
            if g == 0:
                stop_here("e0a")
            S.op("dve", lambda e, b=b, g=g: e.tensor_copy(out=WsTf[:, g, :], in_=psum_tiles[b][:, 0:128]),
                 reads=[pkey(b)], writes=["WsTf"])
            if g == 0:
                stop_here("e0b")
            S.op("act", lambda e, g=g: e.activation(out=WsT[:, g, :], in_=WsTf[:, g, :], func=AF.Copy),
                 reads=["WsTf"], writes=["WsT"])
            PB.free(b)
            if g == 0:
                stop_here("e1")
            S.op("dve", lambda e: e.memset(wsfs[:], 0.0), writes=["wsfs"])

            def wsfs_dma(e, g=g):
                return [e.dma_start(out=wsfs[0:16, 0:16], in_=w_s_d[g, 0:16, 0:16]),
                        e.dma_start(out=wsfs[16:32, 16:32], in_=w_s_d[g, 0:16, 0:16])]
            S.op("sp", wsfs_dma, reads=["wsfs"], writes=["wsfs"], dma="d_c13", ndma=2)
            if g == 0:
                stop_here("e2")
            b = PB.alloc()
            S.op("pe", lambda e, b=b: e.transpose(psum_tiles[b][0:32, 0:32], wsfs[:], identf[0:32, 0:32]),
                 reads=["wsfs", "identf"], writes=[pkey(b)])
            S.op("dve", lambda e, b=b, g=g: e.tensor_copy(out=WsTfs[:, g, :], in_=psum_tiles[b][0:32, 0:32]),
                 reads=[pkey(b)], writes=["WsTfs"])
            S.op("act", lambda e, g=g: e.activation(out=WsTs[:, g, :], in_=WsTfs[:, g, :], func=AF.Copy),
                 reads=["WsTfs"], writes=["WsTs"])
            PB.free(b)
        stop_here("d2")
        for (n, wtf, b2, r2, rb, rbk) in ((128, WsTf, bs2, R2, Rb, "Rb"), (32, WsTfs, bs2s, R2s, Rbs, "Rbs")):
            b = PB.alloc()

            def mm(e, b=b, n=n, wtf=wtf):
                r = None
                for g in range(4):
                    r = e.matmul(psum_tiles[b][0:2, g * n:(g + 1) * n], lhsT=e2col[0:n, :], rhs=wtf[:, g, :],
                                 start=True, stop=True)
                return r
            S.op("pe", mm, reads=["e2col", "WsTf", "WsTfs"], writes=[pkey(b)])
            S.op("dve", lambda e, b=b, n=n, b2=b2, r2=r2: e.tensor_tensor(
                out=r2[:].rearrange("p g i -> p (g i)"), in0=psum_tiles[b][0:2, 0:4 * n],
                in1=b2[:].rearrange("p g i -> p (g i)"), op=ALU.add), reads=[pkey(b), "bs2", "bs2s"], writes=[rbk + "r2"])
            PB.free(b)
            for half in range(2):
                b = PB.alloc()

                def mm2(e, b=b, n=n, half=half, r2=r2):
                    r = None
                    for cc in range(4):
                        c = half * 4 + cc
                        r = e.matmul(psum_tiles[b][:, cc * n:(cc + 1) * n], lhsT=L2t[0:2, c * 128:(c + 1) * 128],
                                     rhs=r2[:, c // 2, :], start=True, stop=True)
                    return r
                S.op("pe", mm2, reads=[L2k, rbk + "r2"], writes=[pkey(b)])
                S.op("dve", lambda e, b=b, n=n, half=half, rb=rb: e.tensor_copy(
                    out=rb[:, half * 4:(half + 1) * 4, :],
                    in_=psum_tiles[b][:, 0:4 * n].rearrange("p (c i) -> p c i", i=n)), reads=[pkey(b)], writes=[rbk])
                PB.free(b)

        stop_here("c5")
        def make_group(name, NT, subs, segs, mods, mask_f, mask_b, wst, rbt, rbk):
            g = Grp()
            g.name, g.NT, g.subs, g.segs, g.mods = name, NT, subs, segs, mods
            g.mask_f, g.mask_b, g.wst, g.rbt, g.rbk = mask_f, mask_b, wst, rbt, rbk
            n = subs[0][1]
            g.n = n
            nsub = len(subs)
            g.hT = sb(f"hT_{name}", [128, 8, NT], BF16)
            g.alrT = sb(f"alrT_{name}", [17, NT], BF16)
            g.E12 = sb(f"E12_{name}", [128, 8, NT])
            g.E1 = g.E12[:, 0:4, :]
            g.E2 = g.E12[:, 4:8, :]
            g.ta = g.E12
            g.qT = sb(f"qT_{name}", [128, 4, NT], BF16)
            g.kT = sb(f"kT_{name}", [128, 4, NT], BF16)
            g.vtok = [sb(f"vx_{name}{i}", [n, D], BF16) for i in range(nsub)]
            g.xhb = g.vtok
            g.on = [sb(f"on_{name}{i}", [n, D], BF16) for i in range(nsub)]
            g.zT = sb(f"zT_{name}", [128, 8, NT], BF16)
            g.uz = sb(f"um_{name}", [128, 8, NT], BF16)
            g.mT = g.uz
            S.op("dve", lambda e: e.memset(g.alrT[:], 1.0), writes=[f"alrT_{name}"])
            return g

        gp = make_group("p", 512, [(i * 128, 128) for i in range(4)], [[(0, 128)]] * 4, [[0]] * 4,
                        masku, maskub, WsT, Rb, "Rb")
        gs = make_group("s", 32, [(0, 32)], [[(0, 16), (16, 16)]], [[1, 2]], maskus, maskusb, WsTs, Rbs, "Rbs")

        Sf_p = sb("Sf_p", [128, 4, 256])
        Sb_p = sb("Sb_p", [128, 4, 256], BF16)
        Dtot = sb("Dtot", [128, 4])
        Sf_s = [sb(f"Sf_s{i}", [128, 4, 256]) for i in range(2)]
        Sb_s = [sb(f"Sb_s{i}", [128, 4, 256], BF16) for i in range(2)]
        S.op("dve", lambda e: e.memset(Sf_p[:], 0.0), writes=["Sf_p"])
        S.op("dve", lambda e: e.memset(Sb_p[:], 0.0), writes=["Sb_p"])
        S.op("dve", lambda e: e.memset(Dtot[:], 1.0), writes=["Dtot"])
        for i in range(2):
            S.op("sp", lambda e, i=i: e.dma_start(out=Sf_s[i][:], in_=s0_d[i].rearrange("h k v -> k h v")),
                 writes=[f"Sf_s{i}"], dma=f"d_s0{i}")
            S.op("act", lambda e, i=i: e.activation(out=Sb_s[i][:].rearrange("p h v -> p (h v)"),
                                                    in_=Sf_s[i][:].rearrange("p h v -> p (h v)"), func=AF.Copy),
                 reads=[f"Sf_s{i}"], writes=[f"Sb_s{i}"])
        gp.Sf, gp.Sb, gp.Skeys = [Sf_p], [Sb_p], ["p"]
        gq = Grp()
        gq.name, gq.NT, gq.subs, gq.segs, gq.mods = "q", 512, gp.subs, gp.segs, gp.mods
        gq.mask_f, gq.mask_b, gq.wst, gq.rbt, gq.rbk, gq.n = masku, maskub, WsT, Rb, "Rb", 128
        gq.hT = gp.zT
        gq.alrT = sb("alrT_q", [17, 512], BF16)
        S.op("dve", lambda e: e.memset(gq.alrT[:], 1.0), writes=["alrT_q"])
        gq.E1 = shareA[:, :].rearrange("p (h t) -> p h t", t=512)
        gq.E2 = gp.uz[:].rearrange("p a b -> p (a b)").bitcast(F32).rearrange("p (h t) -> p h t", t=512)
        gq.kT = gp.qT
        gq.vtok = gp.on
        gq.Sf, gq.Sb, gq.Skeys = gp.Sf, gp.Sb, gp.Skeys
        gs.Sf, gs.Sb, gs.Skeys = Sf_s, Sb_s, ["s0", "s1"]

        out_events = []
        DBG = bool(int(os.environ.get("KDBG", "0")))

        def dbg(name, ap, shape, dt, keys):
            if not DBG:
                return
            d = nc.dram_tensor("dbg_" + name, shape, dt, kind="ExternalOutput").ap()
            out_events.append(S.op("sp", lambda e: e.dma_start(out=d, in_=ap), reads=keys, dma="d_dbg_" + name))

        def rstd_from_ss(st, stk, col0, ncol, n, scale):
            S.op("act", lambda e: e.activation(out=st[0:n, col0:col0 + ncol], in_=st[0:n, col0:col0 + ncol], func=AF.Ln,
                                               scale=scale, bias=EPS), reads=[stk], writes=[stk])
            S.op("act", lambda e: e.activation(out=st[0:n, col0:col0 + ncol], in_=st[0:n, col0:col0 + ncol], func=AF.Exp,
                                               scale=-0.5), reads=[stk], writes=[stk])

        P1_STATE = {"on": False, "Af": None, "Bf": None}

        def stage_h(g, xsrc, row0, only=None):
            nm = g.name
            sub_ids = [si for si in range(len(g.subs)) if only is None or si in (only if isinstance(only, (list, tuple)) else [only])]
            for p0 in range(0, len(sub_ids), 2):
                pair = sub_ids[p0:p0 + 2]
                ctx = {}
                pst = st_ring.get()
                for j, si in enumerate(pair):
                    off, n = g.subs[si]
                    xt, xk = x_ring.get()
                    S.op("sp", lambda e, xt=xt, off=off, n=n: e.dma_start(out=xt[0:n, :],
                                                                          in_=xsrc[row0 + off:row0 + off + n, :]),
                         writes=[xk], dma="d_" + xk)
                    xn, xnk = xn_ring.get()
                    ctx[si] = {"xt": xt, "xk": xk, "st": pst, "off": off, "n": n, "xn": xn, "xnk": xnk, "j": j}
                for si in pair:
                    c = ctx[si]
                    xt, xk, (st, stk), n, xn, xnk, j = c["xt"], c["xk"], c["st"], c["n"], c["xn"], c["xnk"], c["j"]
                    S.op("act", lambda e, xt=xt, st=st, n=n, xn=xn, j=j: e.activation(
                        out=xn[0:n, :], in_=xt[0:n, :], func=AF.Square, accum_out=st[0:n, j:j + 1]),
                        reads=[xk], writes=[stk, xnk])
                (st, stk), n, npair = pst, ctx[pair[0]]["n"], len(pair)
                S.op("act", lambda e, st=st, n=n, npair=npair: e.activation(out=st[0:n, 0:npair], in_=st[0:n, 0:npair],
                                                                            func=AF.Ln, scale=1.0 / D, bias=EPS),
                     reads=[stk], writes=[stk])
                S.op("act", lambda e, st=st, n=n, npair=npair: e.activation(out=st[0:n, 0:npair], in_=st[0:n, 0:npair],
                                                                            func=AF.Exp, scale=-0.5),
                     reads=[stk], writes=[stk])
                for si in pair:
                    c = ctx[si]
                    xt, xk, (st, stk), n = c["xt"], c["xk"], c["st"], c["n"]
                    xn, xnk, j = c["xn"], c["xnk"], c["j"]
                    S.op("act", lambda e, xt=xt, st=st, xn=xn, n=n, j=j: e.activation(out=xn[0:n, :], in_=xt[0:n, :],
                                                                                       func=AF.Copy, scale=st[0:n, j:j + 1]),
                         reads=[xk, stk], writes=[xnk])
                for si in pair:
                    c = ctx[si]
                    xn, xnk, n = c["xn"], c["xnk"], c["n"]
                    b = PB.alloc()
                    c["b"] = b
                    pbf = psum_tiles[b][:].bitcast(BF16)
                    c["pbf"] = pbf

                    def tr(e, xn=xn, n=n, pbf=pbf):
                        r = None
                        for kc in range(8):
                            r = e.transpose(pbf[:, kc * 128:kc * 128 + n], xn[0:n, kc * 128:(kc + 1) * 128],
                                            identb[0:n, 0:n])
                        return r
                    S.op("pe", tr, reads=[xnk, "identb"], writes=[pkey(b)])
                for si in pair:
                    c = ctx[si]
                    b, pbf, off = c["b"], c["pbf"], c["off"]
                    on_act = (c["j"] == 1)
                    if P1_STATE["on"] and not on_act:
                        n = c["n"]
                        pv3 = pbf.rearrange("p (k t) -> p k t", t=128)
                        S.op("dve", lambda e, pv3=pv3, off=off, n=n: e.tensor_tensor(
                            out=g.hT[:, :, off:off + n], in0=pv3[:, :, 0:n], in1=P1_STATE["Af"][:, :, 0:n], op=ALU.mult),
                            reads=[pkey(b), xo_ring.k[0]], writes=[f"hT_{nm}{si}"])
                        S.op("dve", lambda e, off=off, n=n: e.tensor_tensor(
                            out=g.hT[:, :, off:off + n], in0=g.hT[:, :, off:off + n], in1=P1_STATE["Bf"][:, :, 0:n],
                            op=ALU.add), reads=[f"hT_{nm}{si}", xo_ring.k[1]], writes=[f"hT_{nm}{si}"])
                        PB.free(b)
                        continue
                    for kc in range(8):
                        for (s0, sl), m in zip(g.segs[si], g.mods[si]):
                            if on_act:
                                S.op("act", lambda e, kc=kc, s0=s0, sl=sl, m=m, off=off, pbf=pbf: e.activation(
                                    out=g.hT[:, kc, off + s0:off + s0 + sl], in_=pbf[:, kc * 128 + s0:kc * 128 + s0 + sl],
                                    func=AF.Identity, scale=modA[:, m, kc:kc + 1], bias=adaT[:, kc, m:m + 1]),
                                    reads=[pkey(b), "modA", "adaT"], writes=[f"hT_{nm}{si}"])
                            else:
                                S.op("dve", lambda e, kc=kc, s0=s0, sl=sl, m=m, off=off, pbf=pbf: e.tensor_scalar(
                                    out=g.hT[:, kc, off + s0:off + s0 + sl], in0=pbf[:, kc * 128 + s0:kc * 128 + s0 + sl],
                                    scalar1=modA[:, m, kc:kc + 1], scalar2=adaT[:, kc, m:m + 1], op0=ALU.mult, op1=ALU.add),
                                    reads=[pkey(b), "modA", "adaT"], writes=[f"hT_{nm}{si}"])
                    PB.free(b)

        def hT_keys(g):
            return [f"hT_{g.name}{si}" for si in range(len(g.subs))]

        def stage_a(g):
            nm = g.name
            NT = g.NT
            b = PB.alloc()

            def mm(e, b=b):
                r = None
                for kc in range(8):
                    r = e.matmul(psum_tiles[b][0:16, 0:NT], lhsT=wa[:, kc, :], rhs=g.hT[:, kc, :], start=(kc == 0),
                                 stop=(kc == 7))
                return r
            S.op("pe", mm, reads=["wa"] + hT_keys(g), writes=[pkey(b)])
            S.op("dve", lambda e, b=b: e.tensor_copy(out=g.alrT[0:16, :], in_=psum_tiles[b][0:16, 0:NT]),
                 reads=[pkey(b)], writes=[f"alrT_{nm}"])
            PB.free(b)
            nsub = len(g.subs)
            for p0 in range(0, nsub, 2):
                pair = list(range(p0, min(p0 + 2, nsub)))
                ctx = {}
                for si in pair:
                    off, n = g.subs[si]
                    b = PB.alloc()
                    S.op("pe", lambda e, b=b, off=off, n=n: e.matmul(psum_tiles[b][0:n, :], lhsT=g.alrT[:, off:off + n],
                                                                    rhs=wa2[:, :], start=True, stop=True),
                         reads=[f"alrT_{nm}", "wa2"], writes=[pkey(b)])
                    ctx[si] = {"b": b, "off": off, "n": n}
                for si in pair:
                    c = ctx[si]
                    b, n = c["b"], c["n"]
                    et, ek = e_ring.get()
                    c["et"], c["ek"] = et, ek
                    S.op("act", lambda e, b=b, n=n, et=et: e.activation(out=et[0:n, :], in_=psum_tiles[b][0:n, :],
                                                                        func=AF.Exp, scale=-1.0), reads=[pkey(b)], writes=[ek])
                    PB.free(b)
                for si in pair:
                    c = ctx[si]
                    et, ek, n = c["et"], c["ek"], c["n"]
                    spt, spk = sp_ring.get()
                    c["spt"], c["spk"] = spt, spk
                    S.op("act", lambda e, n=n, et=et, spt=spt: e.activation(out=spt[0:n, :], in_=et[0:n, :], func=AF.Ln,
                                                                            bias=1.0), reads=[ek], writes=[spk])
                for si in pair:
                    c = ctx[si]
                    spt, spk, n = c["spt"], c["spk"], c["n"]
                    b = PB.alloc()
                    c["b2"] = b

                    def mmp(e, b=b, n=n, spt=spt):
                        r = None
                        for h in range(4):
                            r = e.matmul(psum_tiles[b][:, h * n:(h + 1) * n], lhsT=spt[0:n, h * 128:(h + 1) * 128],
                                         rhs=g.mask_b[0:n, 0:n], start=True, stop=True)
                        return r
                    S.op("pe", mmp, reads=[spk, "maskub", "maskusb"], writes=[pkey(b)])
                for which, sc in (("E1", -1.0 / 16.0), ("E2", 1.0 / 16.0)):
                    for si in pair:
                        c = ctx[si]
                        b, n, off = c["b2"], c["n"], c["off"]
                        pv = psum_tiles[b][:, 0:4 * n].rearrange("p (h t) -> p h t", t=n)
                        dst = (g.E1 if which == "E1" else g.E2)[:, :, off:off + n]
                        S.op("act", lambda e, pv=pv, dst=dst, sc=sc: e.activation(out=dst, in_=pv, func=AF.Exp, scale=sc),
                             reads=[pkey(b)], writes=[f"{which}_{nm}{si}"] + [f"ta_{nm}{d}" for d in range(8)])
                for si in pair:
                    PB.free(ctx[si]["b2"])

        def Ek(g, which):
            return [f"{which}_{g.name}{si}" for si in range(len(g.subs))]

        def stage_qk(g, slot, is_q):
            nm = g.name
            NT = g.NT
            for h in range(4):
                b = PB.alloc()

                def mm(e, b=b, h=h):
                    r = None
                    for kc in range(8):
                        r = e.matmul(psum_tiles[b][:, 0:NT], lhsT=wbuf[slot][:, kc, h * 128:(h + 1) * 128],
                                     rhs=g.hT[:, kc, :], start=(kc == 0), stop=(kc == 7))
                    return r
                S.op("pe", mm, reads=[f"wbuf{slot}"] + hT_keys(g), writes=[pkey(b)])
                if is_q:
                    S.op("dve", lambda e, b=b, h=h: e.scalar_tensor_tensor(
                        out=g.qT[:, h, :], in0=psum_tiles[b][:, 0:NT], scalar=128.0 ** -0.5, in1=g.E1[:, h, :],
                        op0=ALU.mult, op1=ALU.mult), reads=[pkey(b)] + Ek(g, "E1"), writes=[f"qT_{nm}"])
                else:
                    S.op("dve", lambda e, b=b, h=h: e.tensor_tensor(
                        out=g.kT[:, h, :], in0=psum_tiles[b][:, 0:NT], in1=g.E2[:, h, :], op=ALU.mult),
                        reads=[pkey(b)] + Ek(g, "E2"), writes=[f"kT_{nm}"])
                PB.free(b)

        def tok_major(g, slot, cg, si, evac):
            off, n = g.subs[si]
            b = PB.alloc()

            def mm(e, b=b):
                r = None
                for kc in range(8):
                    r = e.matmul(psum_tiles[b][0:n, :], lhsT=g.hT[:, kc, off:off + n], rhs=wbuf[slot][:, kc, :],
                                 start=(kc == 0), stop=(kc == 7))
                return r
            S.op("pe", mm, reads=[f"wbuf{slot}", f"hT_{g.name}{si}"], writes=[pkey(b)])
            evac(b, n)
            PB.free(b)

        def stage_v(g, slot, cg):
            for si in range(len(g.subs)):
                def ev(b, n, si=si):
                    S.op("act", lambda e: e.activation(out=g.vtok[si][0:n, cg * 512:(cg + 1) * 512],
                                                       in_=psum_tiles[b][0:n, :], func=AF.Copy),
                         reads=[pkey(b)], writes=[f"vx_{g.name}{si}_{cg}"])
                tok_major(g, slot, cg, si, ev)

        def vkeys(g, si):
            return [f"vx_{g.name}{si}_0", f"vx_{g.name}{si}_1"]

        def stage_gla(g, si, full):
            nm = g.name
            off, n = g.subs[si]
            segs = g.segs[si]
            multi = len(segs) > 1
            ktoks = []
            for gi, (s0, sl) in enumerate(segs):
                kh, khk = khT_ring.get()
                if multi:
                    S.op("pool", lambda e, kh=kh: e.memset(kh[:, :, 0:n], 0.0), writes=[khk])
                last = off + s0 + sl - 1
                for h in range(4):
                    S.op("dve", lambda e, kh=kh, h=h, s0=s0, sl=sl, last=last: e.tensor_scalar(
                        out=kh[:, h, s0:s0 + sl], in0=g.kT[:, h, off + s0:off + s0 + sl],
                        scalar1=g.E1[:, h, last:last + 1], scalar2=None, op0=ALU.mult),
                        reads=[f"kT_{nm}", f"E1_{nm}{si}", khk], writes=[khk])
                b = PB.alloc()
                pbf = psum_tiles[b][:].bitcast(BF16)

                def tr(e, kh=kh, pbf=pbf):
                    r = None
                    for h in range(4):
                        r = e.transpose(pbf[0:n, h * 128:(h + 1) * 128], kh[:, h, 0:n], identb[:, :])
                    return r
                S.op("pe", tr, reads=[khk, "identb"], writes=[pkey(b)])
                kt, ktk = ktok_ring.get()
                S.op("act", lambda e, kt=kt, pbf=pbf: e.activation(out=kt[0:n, :], in_=pbf[0:n, 0:512], func=AF.Copy),
                     reads=[pkey(b)], writes=[ktk])
                PB.free(b)
                ktoks.append((kt, ktk))
            if full:
                b = PB.alloc()

                def mma(e, b=b):
                    r = None
                    for h in range(4):
                        r = e.matmul(psum_tiles[b][0:n, h * n:(h + 1) * n], lhsT=g.kT[:, h, off:off + n],
                                     rhs=g.qT[:, h, off:off + n], start=True, stop=True)
                    return r
                S.op("pe", mma, reads=[f"kT_{nm}", f"qT_{nm}"], writes=[pkey(b)])
                at, atk = atm_ring.get()
                for h in range(4):
                    S.op("dve", lambda e, b=b, h=h, at=at: e.tensor_tensor(
                        out=at[0:n, h * n:(h + 1) * n], in0=psum_tiles[b][0:n, h * n:(h + 1) * n], in1=g.mask_f[0:n, 0:n],
                        op=ALU.mult), reads=[pkey(b), "masku", "maskus"], writes=[atk])
                PB.free(b)
                qsegs = []
                for gi, (s0, sl) in enumerate(segs):
                    if multi:
                        qm, qmk = qm_ring.get()
                        S.op("pool", lambda e, qm=qm: e.memset(qm[:, :, 0:n], 0.0), writes=[qmk])
                        S.op("pool", lambda e, qm=qm, s0=s0, sl=sl: e.tensor_copy(
                            out=qm[:, :, s0:s0 + sl], in_=g.qT[:, :, off + s0:off + s0 + sl]),
                            reads=[f"qT_{nm}", qmk], writes=[qmk])
                        qsegs.append((lambda h, qm=qm: qm[:, h, 0:n], qmk))
                    else:
                        qsegs.append((lambda h: g.qT[:, h, off:off + n], f"qT_{nm}"))
                ob = [PB.alloc(), PB.alloc()]

                def mmo(e):
                    r = None
                    for h in range(4):
                        o_ap = psum_tiles[ob[h // 2]][0:n, (h % 2) * 256:(h % 2 + 1) * 256]
                        r = e.matmul(o_ap, lhsT=at[0:n, h * n:(h + 1) * n], rhs=g.vtok[si][0:n, h * 256:(h + 1) * 256],
                                     start=True, stop=False)
                        for gi in range(len(segs)):
                            r = e.matmul(o_ap, lhsT=qsegs[gi][0](h), rhs=g.Sb[gi][:, h, :], start=False,
                                         stop=(gi == len(segs) - 1))
                    return r
                S.op("pe", mmo, reads=[atk] + vkeys(g, si) + [q[1] for q in qsegs] + [f"Sb_{k}" for k in g.Skeys],
                     writes=[pkey(ob[0]), pkey(ob[1])])
                st, stk = st_ring.get()
                for h in range(4):
                    S.op("act", lambda e, h=h, st=st: e.activation(
                        out=g.on[si][0:n, h * 256:(h + 1) * 256],
                        in_=psum_tiles[ob[h // 2]][0:n, (h % 2) * 256:(h % 2 + 1) * 256],
                        func=AF.Square, accum_out=st[0:n, h:h + 1]), reads=[pkey(ob[h // 2])],
                        writes=[stk, f"on_{nm}{si}_{h}"])
                rstd_from_ss(st, stk, 0, 4, n, 1.0 / 256.0)
                for h in range(4):
                    S.op("dve", lambda e, h=h, st=st: e.tensor_scalar(
                        out=g.on[si][0:n, h * 256:(h + 1) * 256],
                        in0=psum_tiles[ob[h // 2]][0:n, (h % 2) * 256:(h % 2 + 1) * 256], scalar1=st[0:n, h:h + 1],
                        scalar2=None, op0=ALU.mult), reads=[pkey(ob[h // 2]), stk], writes=[f"on_{nm}{si}_{h}"])
                PB.free(ob[0])
                PB.free(ob[1])
            for gi, (s0, sl) in enumerate(segs):
                kt, ktk = ktoks[gi]
                last = off + s0 + sl - 1
                ub = [PB.alloc(), PB.alloc()]

                def mmu(e, kt=kt, ub=ub):
                    r = None
                    for h in range(4):
                        r = e.matmul(psum_tiles[ub[h // 2]][:, (h % 2) * 256:(h % 2 + 1) * 256],
                                     lhsT=kt[0:n, h * 128:(h + 1) * 128], rhs=g.vtok[si][0:n, h * 256:(h + 1) * 256],
                                     start=True, stop=True)
                    return r
                S.op("pe", mmu, reads=[ktk] + vkeys(g, si), writes=[pkey(ub[0]), pkey(ub[1])])
                sk = g.Skeys[gi]
                for h in range(4):
                    S.op("dve", lambda e, h=h, gi=gi, last=last, ub=ub: e.scalar_tensor_tensor(
                        out=g.Sf[gi][:, h, :], in0=g.Sf[gi][:, h, :], scalar=g.E1[:, h, last:last + 1],
                        in1=psum_tiles[ub[h // 2]][:, (h % 2) * 256:(h % 2 + 1) * 256], op0=ALU.mult, op1=ALU.add),
                        reads=[pkey(ub[h // 2]), f"E1_{nm}{si}", f"Sf_{sk}"], writes=[f"Sf_{sk}"])
                PB.free(ub[0])
                PB.free(ub[1])
                if full:
                    S.op("act", lambda e, gi=gi: e.activation(out=g.Sb[gi][:].rearrange("p h v -> p (h v)"),
                                                              in_=g.Sf[gi][:].rearrange("p h v -> p (h v)"), func=AF.Copy),
                         reads=[f"Sf_{sk}"], writes=[f"Sb_{sk}"])

        def stage_r(g, slots):
            nm = g.name
            for si in range(len(g.subs)):
                off, n = g.subs[si]
                sr, srk = sr_ring.get()
                for cg in range(2):
                    def ev(b, n, sr=sr, srk=srk, cg=cg):
                        S.op("act", lambda e: e.activation(out=sr[0:n, cg * 512:(cg + 1) * 512], in_=psum_tiles[b][0:n, :],
                                                           func=AF.Silu), reads=[pkey(b)], writes=[srk])
                    tok_major(g, slots[cg], cg, si, ev)
                za, zak = za_ring.get()
                S.op("dve", lambda e, za=za, sr=sr, n=n, si=si: e.tensor_tensor(
                    out=za[0:n, :], in0=g.on[si][0:n, :], in1=sr[0:n, :], op=ALU.mult),
                    reads=[f"on_{nm}{si}_{h}" for h in range(4)] + [srk], writes=[zak])
                b = PB.alloc()
                pbf = psum_tiles[b][:].bitcast(BF16)

                def tr(e, za=za, n=n, pbf=pbf):
                    r = None
                    for c in range(8):
                        r = e.transpose(pbf[:, c * 128:c * 128 + n], za[0:n, c * 128:(c + 1) * 128], identb[0:n, 0:n])
                    return r
                S.op("pe", tr, reads=[zak, "identb"], writes=[pkey(b)])
                pv = pbf.rearrange("p (h f t) -> p h f t", f=2, t=128)
                zv = g.zT[:].rearrange("p (h f) t -> p h f t", f=2)
                for half in range(2):
                    S.op("dve", lambda e, half=half, pv=pv, zv=zv, off=off, n=n: e.tensor_scalar(
                        out=zv[:, :, half, off:off + n], in0=pv[:, :, half, 0:n], scalar1=RT[:, 8 + half:9 + half],
                        scalar2=None, op0=ALU.mult), reads=[pkey(b), "RT"], writes=[f"zT_{nm}{si}"])
                PB.free(b)

        def zT_keys(g):
            return [f"zT_{g.name}{si}" for si in range(len(g.subs))]

        def stage_proj(g, slot_p, slot_g, cg, branch):
            nm = g.name
            NT = g.NT
            for dd in range(4):
                d = cg * 4 + dd
                by = PB.alloc()

                def mmy(e, by=by, dd=dd):
                    r = None
                    for c in range(8):
                        r = e.matmul(psum_tiles[by][:, 0:NT], lhsT=wbuf[slot_p][:, c, dd * 128:(dd + 1) * 128],
                                     rhs=g.zT[:, c, :], start=(c == 0), stop=(c == 7))
                    return r
                S.op("pe", mmy, reads=[f"wbuf{slot_p}"] + zT_keys(g), writes=[pkey(by)])
                bg = PB.alloc()

                def mmg(e, bg=bg, dd=dd):
                    r = None
                    for kc in range(8):
                        r = e.matmul(psum_tiles[bg][:, 0:NT], lhsT=wbuf[slot_g][:, kc, dd * 128:(dd + 1) * 128],
                                     rhs=g.hT[:, kc, :], start=(kc == 0), stop=(kc == 7))
                    return r
                S.op("pe", mmg, reads=[f"wbuf{slot_g}"] + hT_keys(g), writes=[pkey(bg)])
                tg, tgk = tg_ring.get()
                S.op("act", lambda e, bg=bg, tg=tg, d=d: e.activation(out=tg[:, 0:NT], in_=psum_tiles[bg][:, 0:NT],
                                                                      func=AF.Tanh, scale=0.5, bias=hbg[:, branch, d:d + 1]),
                     reads=[pkey(bg), "hbg"], writes=[tgk])
                PB.free(bg)
                if branch == 0:
                    S.op("dve", lambda e, by=by, tg=tg, d=d: e.scalar_tensor_tensor(
                        out=g.ta[:, d, :], in0=tg[:, 0:NT], scalar=1.0, in1=psum_tiles[by][:, 0:NT], op0=ALU.add,
                        op1=ALU.mult), reads=[pkey(by), tgk], writes=[f"ta_{nm}{d}"] + Ek(g, "E1") + Ek(g, "E2"))
                else:
                    tm, tmk = tm_ring.get()
                    S.op("dve", lambda e, by=by, tg=tg, tm=tm: e.scalar_tensor_tensor(
                        out=tm[:, 0:NT], in0=tg[:, 0:NT], scalar=1.0, in1=psum_tiles[by][:, 0:NT], op0=ALU.add,
                        op1=ALU.mult), reads=[pkey(by), tgk], writes=[tmk])
                    S.op("dve", lambda e, tm=tm, d=d: e.tensor_tensor(out=g.mT[:, d, :], in0=tm[:, 0:NT], in1=g.ta[:, d, :],
                                                                      op=ALU.add),
                         reads=[tmk, f"ta_{nm}{d}"], writes=[f"um_{nm}{d}"])
                PB.free(by)

        def stage_gv(g, slots):
            nm = g.name
            for si in range(len(g.subs)):
                off, n = g.subs[si]
                st, stk = st_ring.get()
                banks = []
                for cg in range(2):
                    b = PB.alloc()

                    def mm(e, b=b, off=off, n=n, cg=cg):
                        r = None
                        for kc in range(8):
                            r = e.matmul(psum_tiles[b][0:n, :], lhsT=g.hT[:, kc, off:off + n], rhs=wbuf[slots[cg]][:, kc, :],
                                         start=(kc == 0), stop=(kc == 7))
                        return r
                    S.op("pe", mm, reads=[f"wbuf{slots[cg]}", f"hT_{nm}{si}"], writes=[pkey(b)])
                    banks.append(b)
                    S.op("act", lambda e, b=b, n=n, st=st, cg=cg: e.activation(
                        out=g.xhb[si][0:n, cg * 512:(cg + 1) * 512], in_=psum_tiles[b][0:n, :], func=AF.Identity,
                        accum_out=st[0:n, cg:cg + 1]), reads=[pkey(b)], writes=[stk, f"vx_{nm}{si}_{cg}"])
                    S.op("act", lambda e, b=b, n=n, st=st, cg=cg: e.activation(
                        out=g.xhb[si][0:n, cg * 512:(cg + 1) * 512], in_=psum_tiles[b][0:n, :], func=AF.Square,
                        accum_out=st[0:n, 2 + cg:3 + cg]), reads=[pkey(b)], writes=[stk, f"vx_{nm}{si}_{cg}"])
                S.op("dve", lambda e, st=st, n=n: e.tensor_tensor(out=st[0:n, 4:5], in0=st[0:n, 0:1], in1=st[0:n, 1:2],
                                                                  op=ALU.add), reads=[stk], writes=[stk])
                S.op("dve", lambda e, st=st, n=n: e.tensor_tensor(out=st[0:n, 5:6], in0=st[0:n, 2:3], in1=st[0:n, 3:4],
                                                                  op=ALU.add), reads=[stk], writes=[stk])
                S.op("dve", lambda e, st=st, n=n: e.tensor_scalar(out=st[0:n, 4:6], in0=st[0:n, 4:6], scalar1=1.0 / D,
                                                                  scalar2=None, op0=ALU.mult), reads=[stk], writes=[stk])
                S.op("dve", lambda e, st=st, n=n: e.tensor_tensor(out=st[0:n, 6:7], in0=st[0:n, 4:5], in1=st[0:n, 4:5],
                                                                  op=ALU.mult), reads=[stk], writes=[stk])
                S.op("dve", lambda e, st=st, n=n: e.tensor_tensor(out=st[0:n, 5:6], in0=st[0:n, 5:6], in1=st[0:n, 6:7],
                                                                  op=ALU.subtract), reads=[stk], writes=[stk])
                rstd_from_ss(st, stk, 5, 1, n, 1.0)
                S.op("dve", lambda e, st=st, n=n: e.scalar_tensor_tensor(
                    out=st[0:n, 6:7], in0=st[0:n, 4:5], scalar=-1.0, in1=st[0:n, 5:6], op0=ALU.mult, op1=ALU.mult),
                    reads=[stk], writes=[stk])
                if nm == "s":
                    vnf, vnk = xo_ring.get()
                    lwt, lwk = x_ring.get()
                    lbt, lbk = x_ring.get()
                    S.op("sp", lambda e, lwt=lwt, n=n: e.dma_start(out=lwt[0:n, :], in_=bass.AP(lnw_h, 0, [[0, n], [1, D]])),
                         writes=[lwk], dma="d_" + lwk)
                    S.op("sp", lambda e, lbt=lbt, n=n: e.dma_start(out=lbt[0:n, :], in_=bass.AP(lnb_h, 0, [[0, n], [1, D]])),
                         writes=[lbk], dma="d_" + lbk)
                for c2, bb in enumerate(banks):
                    S.op("act", lambda e, c2=c2, bb=bb, n=n, st=st, si=si: e.activation(
                        out=g.xhb[si][0:n, c2 * 512:(c2 + 1) * 512], in_=psum_tiles[bb][0:n, :], func=AF.Identity,
                        scale=st[0:n, 5:6], bias=st[0:n, 6:7]), reads=[pkey(bb), stk], writes=[f"vx_{nm}{si}_{c2}"])
                    if nm == "s":
                        S.op("act", lambda e, c2=c2, bb=bb, n=n, st=st, vnf=vnf: e.activation(
                            out=vnf[0:n, c2 * 512:(c2 + 1) * 512], in_=psum_tiles[bb][0:n, :], func=AF.Identity,
                            scale=st[0:n, 5:6], bias=st[0:n, 6:7]), reads=[pkey(bb), stk], writes=[vnk])
                    PB.free(bb)
                if nm == "s":
                    S.op("dve", lambda e, n=n, vnf=vnf, lwt=lwt: e.tensor_tensor(out=vnf[0:n, :], in0=vnf[0:n, :],
                                                                                 in1=lwt[0:n, :], op=ALU.mult),
                         reads=[vnk, lwk], writes=[vnk])
                    S.op("dve", lambda e, n=n, vnf=vnf, lbt=lbt: e.tensor_tensor(out=vnf[0:n, :], in0=vnf[0:n, :],
                                                                                 in1=lbt[0:n, :], op=ALU.add),
                         reads=[vnk, lbk], writes=[vnk])
                    out_events.append(S.op("sp", lambda e, n=n, vnf=vnf: e.dma_start(out=vn_d[0:n, :], in_=vnf[0:n, :]),
                                           reads=[vnk], dma="d_vn"))

        def stage_uz(g, slot_u, slot_z, cg):
            nm = g.name
            NT = g.NT
            for cc in range(4):
                c = cg * 4 + cc
                bu = PB.alloc()
                bz = PB.alloc()

                def mm(e, bank, slot, cc=cc):
                    r = None
                    for kc in range(8):
                        r = e.matmul(psum_tiles[bank][:, 0:NT], lhsT=wbuf[slot][:, kc, cc * 128:(cc + 1) * 128],
                                     rhs=g.hT[:, kc, :], start=(kc == 0), stop=(kc == 7))
                    return r
                S.op("pe", lambda e, bu=bu, mm=mm: mm(e, bu, slot_u), reads=[f"wbuf{slot_u}"] + hT_keys(g), writes=[pkey(bu)])
                S.op("pe", lambda e, bz=bz, mm=mm: mm(e, bz, slot_z), reads=[f"wbuf{slot_z}"] + hT_keys(g), writes=[pkey(bz)])
                sz, szk = sz_ring.get()
                S.op("act", lambda e, bz=bz, sz=sz: e.activation(out=sz[:, 0:NT], in_=psum_tiles[bz][:, 0:NT], func=AF.Silu),
                     reads=[pkey(bz)], writes=[szk])
                PB.free(bz)
                S.op("dve", lambda e, bu=bu, sz=sz, c=c: e.tensor_tensor(out=g.uz[:, c, :], in0=psum_tiles[bu][:, 0:NT],
                                                                         in1=sz[:, 0:NT], op=ALU.mult),
                     reads=[pkey(bu), szk], writes=[f"um_{nm}{c}"])
                PB.free(bu)

        def stage_spatial(g):
            nm = g.name
            for si in range(len(g.subs)):
                off, n = g.subs[si]
                for half in range(2):
                    b = PB.alloc()

                    def mm(e, b=b, half=half, si=si, n=n):
                        r = None
                        for cc in range(4):
                            c = half * 4 + cc
                            r = e.matmul(psum_tiles[b][:, cc * n:(cc + 1) * n], lhsT=g.xhb[si][0:n, c * 128:(c + 1) * 128],
                                         rhs=g.wst[0:n, c // 2, 0:n], start=True, stop=True)
                        return r
                    S.op("pe", mm, reads=[f"vx_{nm}{si}_0", f"vx_{nm}{si}_1", "WsT", "WsTs"], writes=[pkey(b)])
                    s1, s1k = s1_ring.get()
                    for cc in range(4):
                        c = half * 4 + cc
                        S.op("dve", lambda e, b=b, cc=cc, c=c, s1=s1, n=n: e.scalar_tensor_tensor(
                            out=s1[:, cc * n:(cc + 1) * n], in0=psum_tiles[b][:, cc * n:(cc + 1) * n],
                            scalar=RT[:, 10 + c:11 + c], in1=g.rbt[:, c, 0:n], op0=ALU.mult, op1=ALU.add),
                            reads=[pkey(b), "RT", g.rbk], writes=[s1k])
                    PB.free(b)
                    S.op("dve", lambda e, s1=s1, half=half, off=off, n=n: e.tensor_tensor(
                        out=g.zT[:, half * 4:(half + 1) * 4, off:off + n],
                        in0=s1[:, 0:4 * n].rearrange("p (c i) -> p c i", i=n),
                        in1=g.uz[:, half * 4:(half + 1) * 4, off:off + n], op=ALU.mult),
                        reads=[s1k] + [f"um_{nm}{half * 4 + cc}" for cc in range(4)], writes=[f"zT_{nm}{si}"])

        def stage_out(g, slots, xsrc, row0, ydst):
            nm = g.name
            g05, g05k = (g05p, "g05p") if nm == "p" else (g05s, "g05s")
            for si in range(len(g.subs)):
                off, n = g.subs[si]
                xr, xrk = xr_ring.get()
                S.op("sp", lambda e, xr=xr, off=off, n=n: e.dma_start(out=xr[0:n, :], in_=xsrc[row0 + off:row0 + off + n, :]),
                     writes=[xrk], dma="d_" + xrk)
                xo, xok = xo_ring.get()
                for cg in range(2):
                    b = PB.alloc()

                    def mm(e, b=b, off=off, n=n, cg=cg):
                        r = None
                        for dch in range(8):
                            r = e.matmul(psum_tiles[b][0:n, :], lhsT=g.mT[:, dch, off:off + n],
                                         rhs=wbuf[slots[cg]][:, dch, :], start=(dch == 0), stop=(dch == 7))
                        return r
                    S.op("pe", mm, reads=[f"wbuf{slots[cg]}"] + [f"um_{nm}{d}" for d in range(8)], writes=[pkey(b)])
                    S.op("dve", lambda e, b=b, n=n, xo=xo, cg=cg: e.tensor_tensor(
                        out=xo[0:n, cg * 512:(cg + 1) * 512], in0=psum_tiles[b][0:n, :],
                        in1=g05[0:n, cg * 512:(cg + 1) * 512], op=ALU.mult), reads=[pkey(b), g05k], writes=[xok])
                    PB.free(b)
                S.op("dve", lambda e, n=n, xo=xo, xr=xr: e.tensor_tensor(out=xo[0:n, :], in0=xo[0:n, :], in1=xr[0:n, :],
                                                                         op=ALU.add),
                     reads=[xok, xrk], writes=[xok])
                st, stk = st_ring.get()
                scr, scrk = xn_ring.get()
                S.op("act", lambda e, n=n, xo=xo, st=st, scr=scr: e.activation(out=scr[0:n, :], in_=xo[0:n, :],
                                                                               func=AF.Square, accum_out=st[0:n, 0:1]),
                     reads=[xok], writes=[stk, scrk])
                rstd_from_ss(st, stk, 0, 1, n, 1.0 / D)
                S.op("dve", lambda e, n=n, xo=xo, xr=xr, st=st: e.scalar_tensor_tensor(
                    out=xr[0:n, :], in0=xo[0:n, :], scalar=st[0:n, 0:1], in1=fnw_bc[0:n, :], op0=ALU.mult, op1=ALU.mult),
                    reads=[xok, stk, "fnw_bc", xrk], writes=[xrk])
                out_events.append(S.op("sp", lambda e, n=n, xr=xr, off=off: e.dma_start(
                    out=ydst[row0 + off:row0 + off + n, :], in_=xr[0:n, :]), reads=[xrk], dma="d_y" + xrk))

        class _Stop(Exception):
            pass

        def finish():
            evs = list(out_events) + [(k, S.count[k]) for k in S.semkeys]
            S.final_waits("sp", evs)

        try:
          if KSTOP == "prologue":
            raise _Stop()
          w_release_all()
          slot_k = next_w(("k", 0))
          slots_v = [next_w(("v", 0)), next_w(("v", 1))]
          def p1_tail(g, t):
              S.op("dve", lambda e, t=t: e.tensor_scalar(out=Sf_p[:].rearrange("p h v -> p (h v)"),
                                                         in0=Sf_p[:].rearrange("p h v -> p (h v)"),
                                                         scalar1=pmaskt[:, t:t + 1], scalar2=None, op0=ALU.mult),
                   reads=["Sf_p", "pmaskt"], writes=["Sf_p"])

          p1g = [gp, gq]
          Af = xo_ring.t[0][:, :].rearrange("p (k t) -> p k t", t=128)
          Bf = xo_ring.t[1][:, :].rearrange("p (k t) -> p k t", t=128)
          for kc in range(8):
              S.op("dve", lambda e, kc=kc: e.tensor_scalar(out=Af[:, kc, :], in0=onesf[:, :], scalar1=modA[:, 0, kc:kc + 1],
                                                           scalar2=None, op0=ALU.mult),
                   reads=["onesf", "modA"], writes=[xo_ring.k[0]])
              S.op("dve", lambda e, kc=kc: e.tensor_scalar(out=Bf[:, kc, :], in0=onesf[:, :], scalar1=adaT[:, kc, 0:1],
                                                           scalar2=None, op0=ALU.mult),
                   reads=["onesf", "adaT"], writes=[xo_ring.k[1]])
          P1_STATE["on"], P1_STATE["Af"], P1_STATE["Bf"] = True, Af, Bf
          for t in range(NPREV + 1):
              g = p1g[t % 2]
              gprev = p1g[(t - 1) % 2]
              cur = t < NPREV
              prev = t >= 1
              if cur:
                  stage_h(g, xprev_d, t * 512)
              if prev:
                  stage_gla(gprev, 0, False)
              if cur:
                  stage_a(g)
              if prev:
                  stage_gla(gprev, 1, False)
              if cur:
                  stage_v(g, slots_v[0], 0)
              if prev:
                  stage_gla(gprev, 2, False)
              if cur:
                  stage_v(g, slots_v[1], 1)
              if prev:
                  stage_gla(gprev, 3, False)
                  p1_tail(gprev, t - 1)
              if cur:
                  stage_qk(g, slot_k, False)
          P1_STATE["on"] = False
          w_release_all()
          S.op("act", lambda e: e.activation(out=Sb_p[:].rearrange("p h v -> p (h v)"),
                                             in_=Sf_p[:].rearrange("p h v -> p (h v)"), func=AF.Copy),
               reads=["Sf_p"], writes=["Sb_p"])
          S.barrier()
          gate_bc(g05p, "g05p", 128, [(0, 128, 0)])
          S.op("sp", lambda e: e.dma_start(out=fnw_bc, in_=bass.AP(fnw_h, 0, [[0, 128], [1, D]])),
               writes=["fnw_bc"], dma="d_c6")
          stop_here("phase1")
          stop_here("exchange")
          for t in range(NTILE):
              groups = [(gp, x_d, t * 512, y_d)]
              if t == 0:
                  groups.append((gs, xs_d, 0, ys_d))
              for (g, xsrc, row0, ydst) in groups:
                  if g is gp and t > 0:
                      continue
                  stage_h(g, xsrc, row0)
                  stage_a(g)
              if t == 0:
                  stop_here("s_a")
              slot = next_w(("q", 0))
              for (g, *_r) in groups:
                  stage_qk(g, slot, True)
              w_release_all()
              if t == 0:
                  stop_here("s_q")
              slot = next_w(("k", 0))
              for (g, *_r) in groups:
                  stage_qk(g, slot, False)
              w_release_all()
              if t == 0:
                  stop_here("s_k")
              for cg in range(2):
                  slot = next_w(("v", cg))
                  for (g, *_r) in groups:
                      stage_v(g, slot, cg)
                  w_release_all()
              if t == 0:
                  stop_here("s_v")
              for (g, *_r) in groups:
                  for si in range(len(g.subs)):
                      stage_gla(g, si, True)
              if t == 0:
                  stop_here("s_gla")
              slots = [next_w(("r", 0)), next_w(("r", 1))]
              for (g, *_r) in groups:
                  stage_r(g, slots)
              w_release_all()
              if t == 0:
                  stop_here("s_r")
              for cg in range(2):
                  slot_p = next_w(("pa", cg))
                  slot_g = next_w(("ga", cg))
                  for (g, *_r) in groups:
                      stage_proj(g, slot_p, slot_g, cg, 0)
                  w_release_all()
              if t == 0:
                  stop_here("s_pa")
              slots = [next_w(("gv", 0)), next_w(("gv", 1))]
              for (g, *_r) in groups:
                  stage_gv(g, slots)
              w_release_all()
              if t == 0:
                  stop_here("s_gv")
              for cg in range(2):
                  slot_u = next_w(("u", cg))
                  slot_z = next_w(("z", cg))
                  for (g, *_r) in groups:
                      stage_uz(g, slot_u, slot_z, cg)
                  w_release_all()
              if t == 0:
                  stop_here("s_uz")
              for (g, *_r) in groups:
                  stage_spatial(g)
              if t == 0:
                  stop_here("s_sp")
              for cg in range(2):
                  slot_p = next_w(("pb", cg))
                  slot_g = next_w(("gb", cg))
                  for (g, *_r) in groups:
                      stage_proj(g, slot_p, slot_g, cg, 1)
                  w_release_all()
              if t == 0:
                  stop_here("s_pb")
              slots = [next_w(("wo", 0)), next_w(("wo", 1))]
              if t + 1 < NTILE:
                  stage_h(gp, x_d, (t + 1) * 512)
                  stage_a(gp)
              for (g, xsrc, row0, ydst) in groups:
                  stage_out(g, slots, xsrc, row0, ydst)
              w_release_all()
              if t == 0:
                  dbg("hT", gp.hT[:], [128, 8, 512], BF16, hT_keys(gp))
                  dbg("qT", gp.qT[:], [128, 4, 512], BF16, ["qT_p"])
                  dbg("kT", gp.kT[:], [128, 4, 512], BF16, ["kT_p"])
                  dbg("on0", gp.on[0][:], [128, D], BF16, ["on_p0"])
                  dbg("on1", gp.on[1][:], [128, D], BF16, ["on_p1"])
                  dbg("ta", gp.ta[:], [128, 8, 512], F32, [f"ta_p{d}" for d in range(8)])
                  dbg("zbT", gp.zT[:], [128, 8, 512], BF16, zT_keys(gp))
                  dbg("mT", gp.mT[:], [128, 8, 512], BF16, [f"um_p{d}" for d in range(8)])
                  dbg("xh0", gp.xhb[0][:], [128, D], BF16, ["vx_p0_0", "vx_p0_1"])
                  dbg("g05p", g05p[:], [128, D], F32, ["g05p"])
                  for i in range(2):
                      out_events.append(S.op("sp", lambda e, i=i: e.dma_start(out=ss_d[i], in_=Sf_s[i][:]),
                                             reads=[f"Sf_s{i}"], dma="d_sso"))
          out_events.append(S.op("sp", lambda e: e.dma_start(out=sp_d, in_=Sf_p[:]), reads=["Sf_p"], dma="d_spo"))
          finish()
        except _Stop:
          finish()

        sems = {k: es.enter_context(nc.semaphore(k)) for k in S.semkeys}
        with nc.Block() as block:
            @block.tensor
            def _(e):
                S.run("pe", e, sems)

            @block.scalar
            def _(e):
                S.run("act", e, sems)

            @block.vector
            def _(e):
                S.run("dve", e, sems)

            @block.gpsimd
            def _(e):
                S.run("pool", e, sems)

            @block.sync
            def _(e):
                S.run("sp", e, sems)
    return nc


_CACHE = {}


def _consts():
    ident = np.eye(128, dtype=np.float32)
    s = np.arange(128)
    masku = (s[:, None] <= s[None, :]).astype(np.float32)
    s32 = np.arange(32)
    maskus = ((s32[:, None] <= s32[None, :]) & ((s32[:, None] // 16) == (s32[None, :] // 16))).astype(np.float32)
    idx = s // 64
    wsmask = (idx[:, None] >= idx[None, :]).astype(np.float32)
    return ident, masku, maskus, wsmask


def kernel(x_prompt, x_sample, state_gla, c_prompt, c_sample, norm_w, w_ada, b_ada, w_in, w_a2, b_a,
           gla_norm_w, ln_v_w, ln_v_b, w_s, b_s, b_gate, w_proj_a, w_proj_b, w_out, final_norm_w):
    f = lambda a: np.ascontiguousarray(np.asarray(a, dtype=np.float32))
    x_prompt, x_sample, state_gla = f(x_prompt), f(x_sample), f(state_gla)
    c_prompt, c_sample = f(c_prompt), f(c_sample)
    if "nc" not in _CACHE:
        _CACHE["nc"] = build_nc()
    nc = _CACHE["nc"]
    ident, masku, maskus, wsmask = _consts()
    shared = {
        "norm_w": f(norm_w).reshape(1, D), "w_ada": f(w_ada)[0], "b_ada": f(b_ada).reshape(1, 3 * D),
        "w_in": f(w_in)[0], "w_a2": f(w_a2)[0], "b_a": f(b_a).reshape(1, 512), "gnw": f(gla_norm_w).reshape(1, 256),
        "lnw": f(ln_v_w).reshape(1, D), "lnb": f(ln_v_b).reshape(1, D), "w_s": f(w_s)[0], "b_s": f(b_s)[0],
        "b_gate": f(b_gate)[0], "wpa": f(w_proj_a)[0], "wpb": f(w_proj_b)[0], "wo": f(w_out)[0],
        "fnw": f(final_norm_w).reshape(1, D), "ident": ident, "masku": masku, "maskus": maskus, "wsmask": wsmask,
    }
    in_maps = []
    for c in range(NCORES):
        m = dict(shared)
        m["x"] = x_prompt[0, c * TOK:(c + 1) * TOK, :]
        m["xs"] = x_sample[2 * c:2 * c + 2].reshape(32, D)
        m["s0"] = state_gla[0, 2 * c:2 * c + 2]
        m["c3"] = np.concatenate([c_prompt[0:1], c_sample[2 * c:2 * c + 2]], axis=0)
        npv = 28 * 512
        xp = np.zeros((npv, D), np.float32)
        if c > 0:
            xp[npv - c * TOK:] = x_prompt[0, :c * TOK, :]
        m["xprev"] = xp
        pm = np.zeros((128, 28), np.float32)
        pm[:, 28 - 4 * c:] = 1.0
        m["pmask"] = pm
        in_maps.append(m)
    res = run_bass_kernel_spmd(nc, in_maps, core_ids=list(range(NCORES)))
    R = res.results
    y_prompt = np.concatenate([R[c]["y"] for c in range(NCORES)], axis=0)[None]
    y_sample = np.concatenate([R[c]["ys"].reshape(2, 16, D) for c in range(NCORES)], axis=0)
    sp = np.transpose(R[NCORES - 1]["sp_out"], (1, 0, 2))[None, None]
    ss = np.concatenate([np.transpose(R[c]["ss_out"], (0, 2, 1, 3)) for c in range(NCORES)], axis=0)[None]
    vn = np.concatenate([R[c]["vn_out"].reshape(2, 16, D) for c in range(NCORES)], axis=0)[None]
    return (y_prompt.astype(np.float32), y_sample.astype(np.float32), np.ascontiguousarray(sp, dtype=np.float32),
            np.ascontiguousarray(ss, dtype=np.float32), vn.astype(np.float32))
```

```python
import os
import numpy as np
from contextlib import ExitStack
import concourse.bass as bass
import concourse.mybir as mybir
from concourse.bass_utils import run_bass_kernel_spmd

F32 = mybir.dt.float32
BF16 = mybir.dt.bfloat16
AF = mybir.ActivationFunctionType
ALU = mybir.AluOpType

NCORES = 8
D = 1024
TOK = 2048
NTILE = 4
EPS = 1e-6
C_Q, C_K, C_V, C_R, C_A, C_U, C_GV, C_Z, C_GA, C_GB = 0, 512, 1024, 2048, 3072, 3088, 4112, 5136, 6160, 7184
PROJ_COLS = 8208


class Sched:
    def __init__(self, nc, same_engine_sync=("act", "dve", "pool")):
        self.nc = nc
        self.eng = {"pe": nc.tensor, "act": nc.scalar, "dve": nc.vector,
                    "pool": nc.gpsimd, "sp": nc.sync}
        self.streams = {e: [] for e in self.eng}
        self.count = {}
        self.known = {e: {} for e in self.eng}
        self.last_write = {}
        self.readers = {}
        self.same_engine_sync = set(same_engine_sync)
        self.semkeys = []

    def op(self, engine, fn, reads=(), writes=(), dma=None, ndma=1):
        if getattr(self, "stopped", False):
            return ("pe", 0)

        def _expand(keys):
            out = []
            for k in keys:
                if k.startswith("wbuf") and "_" not in k:
                    out += [f"{k}_{q}" for q in range(4)]
                else:
                    out.append(k)
            return out
        reads, writes = _expand(reads), _expand(writes)
        deps = {}

        def add(ev):
            if ev is not None and deps.get(ev[0], 0) < ev[1]:
                deps[ev[0]] = ev[1]

        for k in reads:
            add(self.last_write.get(k))
        for k in writes:
            add(self.last_write.get(k))
            for r in self.readers.get(k, ()):
                add(r)
        waits = []
        kn = self.known[engine]
        for k, v in deps.items():
            if dma is None and k == engine and engine not in self.same_engine_sync:
                continue
            if kn.get(k, 0) >= v:
                continue
            kn[k] = v
            waits.append((k, v))
        semkey, amt = (engine, 1) if dma is None else (dma, 16 * ndma)
        if ndma == -1:
            amt = 1
        if semkey not in self.count:
            self.count[semkey] = 0
            self.semkeys.append(semkey)
        self.count[semkey] += amt
        ev = (semkey, self.count[semkey])
        self.streams[engine].append((waits, fn, semkey, (ndma if dma is not None else 0)))
        for k in reads:
            self.readers.setdefault(k, []).append(ev)
        for k in writes:
            self.last_write[k] = ev
            self.readers[k] = []
        return ev

    def barrier(self):
        evs = [(k, self.count[k]) for k in self.semkeys if self.count[k] > 0]
        for e in self.eng:
            waits = []
            for k, v in evs:
                if self.known[e].get(k, 0) < v:
                    self.known[e][k] = v
                    waits.append((k, v))
            self.streams[e].append((waits, None, None, 0))

    def final_waits(self, engine, events):
        best = {}
        for k, v in events:
            best[k] = max(best.get(k, 0), v)
        self.streams[engine].append((list(best.items()), None, None, 0))

    def run(self, engine, e, sems):
        for waits, fn, semkey, ndma in self.streams[engine]:
            for k, v in waits:
                e.wait_ge(sems[k], v)
            if fn is None:
                continue
            r = fn(e)
            if ndma == -1:
                r.then_inc(sems[semkey], 1)
            elif ndma:
                if not isinstance(r, (list, tuple)):
                    r = [r]
                assert len(r) == ndma, (len(r), ndma)
                for ins in r:
                    ins.then_inc(sems[semkey], 16)
            else:
                r.then_inc(sems[semkey], 1)


class Bank:
    def __init__(self, tiles):
        self.tiles = tiles
        self.free_list = list(range(len(tiles)))

    def alloc(self):
        assert self.free_list, "out of PSUM banks"
        return self.free_list.pop(0)

    def free(self, i):
        self.free_list.append(i)


class Grp:
    pass


def build_nc():
    nc = bass.Bass("TRN2", target_bir_lowering=False)

    def din(name, shape):
        return nc.dram_tensor(name, shape, F32, kind="ExternalInput").ap()

    def dout(name, shape):
        return nc.dram_tensor(name, shape, F32, kind="ExternalOutput").ap()

    x_d = din("x", [TOK, D])
    xs_d = din("xs", [32, D])
    s0_d = din("s0", [2, 4, 128, 256])
    c3_d = din("c3", [3, D])
    norm_w_d = din("norm_w", [1, D])
    w_ada_d = din("w_ada", [D, 3 * D])
    b_ada_d = din("b_ada", [1, 3 * D])
    w_in_d = din("w_in", [D, PROJ_COLS])
    w_a2_d = din("w_a2", [16, 512])
    b_a_d = din("b_a", [1, 512])
    gnw_d = din("gnw", [1, 256])
    lnw_h = nc.dram_tensor("lnw", [1, D], F32, kind="ExternalInput")
    lnb_h = nc.dram_tensor("lnb", [1, D], F32, kind="ExternalInput")
    lnw_d, lnb_d = lnw_h.ap(), lnb_h.ap()
    w_s_d = din("w_s", [4, 128, 128])
    b_s_d = din("b_s", [4, 128])
    b_gate_d = din("b_gate", [2, D])
    wpa_d = din("wpa", [D, D])
    wpb_d = din("wpb", [D, D])
    wo_d = din("wo", [D, D])
    fnw_h = nc.dram_tensor("fnw", [1, D], F32, kind="ExternalInput")
    fnw_d = fnw_h.ap()
    NPREV = 28
    xprev_d = din("xprev", [NPREV * 512, D])
    pmask_d = din("pmask", [128, NPREV])
    ident_d = din("ident", [128, 128])
    masku_d = din("masku", [128, 128])
    maskus_d = din("maskus", [32, 32])
    wsmask_d = din("wsmask", [128, 128])

    y_d = dout("y", [TOK, D])
    ys_d = dout("ys", [32, D])
    sp_d = dout("sp_out", [128, 4, 256])
    ss_d = dout("ss_out", [2, 128, 4, 256])
    vn_d = dout("vn_out", [32, D])

    ag_in = nc.dram_tensor("ag_in", [128, 1028], F32)
    ag_out = nc.dram_tensor("ag_out", [NCORES * 128, 1028], F32)

    S = Sched(nc)
    es = ExitStack()
    with es:
        def sb(name, shape, dt=F32):
            return es.enter_context(nc.sbuf_tensor("s_" + name, shape, dt))

        psum_tiles = [es.enter_context(nc.psum_tensor(f"pb{i}", [128, 512], F32)) for i in range(8)]
        PB = Bank(psum_tiles)

        def pkey(i):
            return f"pb{i}"

        identf = sb("identf", [128, 128])
        identb = sb("identb", [128, 128], BF16)
        masku = sb("masku", [128, 128])
        maskub = sb("maskub", [128, 128], BF16)
        maskus = sb("maskus", [32, 32])
        maskusb = sb("maskusb", [32, 32], BF16)
        wsmask = sb("wsmask", [128, 128])
        pmaskt = sb("pmaskt", [128, 28])
        onesf = sb("onesf", [128, 128])
        Rrows = sb("Rrows", [82, 128])
        RT = sb("RT", [128, 82])
        scT = sb("scT", [128, 3, 8], BF16)
        adaT = sb("adaT", [128, 24, 3])
        modA = sb("modA", [128, 3, 8])
        hbg = sb("hbg", [128, 2, 8])
        shareA = sb("shareA", [128, 2 * D])
        g05p = shareA[:, 0:D]
        fnw_bc = shareA[:, D:2 * D]
        g05s = sb("g05s", [32, D])
        wa = sb("wa", [128, 8, 16], BF16)
        wa2f = sb("wa2f", [17, 512])
        wa2 = sb("wa2", [17, 512], BF16)
        wsf = sb("wsf", [128, 128])
        WsT = sb("WsT", [128, 4, 128], BF16)
        WsTf = sb("WsTf", [128, 4, 128])
        wsfs = sb("wsfs", [32, 32])
        WsTs = sb("WsTs", [32, 4, 32], BF16)
        WsTfs = sb("WsTfs", [32, 4, 32])
        e2col = sb("e2col", [128, 2])
        bs2 = sb("bs2", [2, 4, 128])
        bs2s = sb("bs2s", [2, 4, 32])
        R2 = sb("R2", [2, 4, 128])
        R2s = sb("R2s", [2, 4, 32])
        Rb = sb("Rb", [128, 8, 128])
        Rbs = sb("Rbs", [128, 8, 32])

        NW = 3
        wbuf = [sb(f"wbuf{i}", [128, 8, 512], BF16) for i in range(NW)]

        class Ring:
            def __init__(self, name, shape, dt, nbuf):
                self.t = [sb(f"{name}{i}", shape, dt) for i in range(nbuf)]
                self.k = [f"{name}{i}" for i in range(nbuf)]
                self.i = 0

            def get(self):
                j = self.i % len(self.t)
                self.i += 1
                return self.t[j], self.k[j]

        x_ring = Ring("xld", [128, D], F32, 2)
        wst_ring = Ring("wst", [128, 2, 512], F32, 4)
        xr_ring = x_ring
        xo_ring = Ring("xo", [128, D], F32, 2)
        b1024 = Ring("b1k", [128, D], BF16, 3)
        xn_ring = b1024
        sr_ring = b1024
        za_ring = b1024
        st_ring = Ring("stat", [128, 8], F32, 6)
        f512 = Ring("f512", [128, 512], F32, 4)
        e_ring = f512
        tg_ring = f512
        tm_ring = f512
        sz_ring = f512
        s1_ring = f512
        sp_ring = Ring("spb", [128, 512], BF16, 2)
        khT_ring = Ring("khT", [128, 4, 128], BF16, 2)
        qm_ring = Ring("qm", [128, 4, 32], BF16, 2)
        ktok_ring = Ring("ktok", [128, 512], BF16, 2)
        atm_ring = Ring("atm", [128, 512], BF16, 2)

        GLt, GLk = xo_ring.t[0], xo_ring.k[0]
        GL = GLt[:].rearrange("p (j t) -> p j t", t=128)
        L2t, L2k = xo_ring.t[1], xo_ring.k[1]
        L2 = L2t[0:2, :]

        def sp_load(out_ap, in_ap, key, sem):
            return S.op("sp", lambda e: e.dma_start(out=out_ap, in_=in_ap), writes=[key], dma=sem)

        KSTOP = os.environ.get("KSTOP", "")

        def stop_here(tag):
            if KSTOP == tag:
                S.stopped = True

        sp_load(identf[:], ident_d, "identf", "d_c0")
        sp_load(masku[:], masku_d, "masku", "d_c1")
        sp_load(maskus[:], maskus_d, "maskus", "d_c2")
        sp_load(wsmask[:], wsmask_d, "wsmask", "d_c3")
        sp_load(pmaskt[:], pmask_d, "pmaskt", "d_c4")
        S.op("dve", lambda e: e.tensor_copy(out=identb[:], in_=identf[:]), reads=["identf"], writes=["identb"])
        S.op("dve", lambda e: e.tensor_copy(out=maskub[:], in_=masku[:]), reads=["masku"], writes=["maskub"])
        S.op("dve", lambda e: e.tensor_copy(out=maskusb[:], in_=maskus[:]), reads=["maskus"], writes=["maskusb"])
        S.op("dve", lambda e: e.memset(onesf[:], 1.0), writes=["onesf"])

        stop_here("c0")
        def rows_dma(e):
            r = []
            r.append(e.dma_start(out=Rrows[0:8, :], in_=norm_w_d.rearrange("o (k p) -> (o k) p", p=128)))
            r.append(e.dma_start(out=Rrows[8:10, :], in_=gnw_d.rearrange("o (k p) -> (o k) p", p=128)))
            r.append(e.dma_start(out=Rrows[10:18, :], in_=lnw_d.rearrange("o (k p) -> (o k) p", p=128)))
            r.append(e.dma_start(out=Rrows[18:34, :], in_=b_gate_d.rearrange("o (k p) -> (o k) p", p=128)))
            r.append(e.dma_start(out=Rrows[34:58, :], in_=b_ada_d.rearrange("o (k p) -> (o k) p", p=128)))
            r.append(e.dma_start(out=Rrows[58:82, :], in_=c3_d.rearrange("o (k p) -> (o k) p", p=128)))
            return r
        S.op("sp", rows_dma, writes=["Rrows"], dma="d_rows", ndma=6)
        b = PB.alloc()
        S.op("pe", lambda e, b=b: e.transpose(psum_tiles[b][:, 0:82], Rrows[:, :], identf[0:82, 0:82]),
             reads=["Rrows", "identf"], writes=[pkey(b)])
        S.op("dve", lambda e, b=b: e.tensor_copy(out=RT[:], in_=psum_tiles[b][:, 0:82]), reads=[pkey(b)], writes=["RT"])
        PB.free(b)
        S.op("act", lambda e: e.activation(out=scT[:].rearrange("p m k -> p (m k)"), in_=RT[:, 58:82], func=AF.Silu),
             reads=["RT"], writes=["scT"])
        S.op("dve", lambda e: e.tensor_scalar(out=hbg[:].rearrange("p m k -> p (m k)"), in0=RT[:, 18:34], scalar1=0.5,
                                              scalar2=None, op0=ALU.mult), reads=["RT"], writes=["hbg"])

        stop_here("c1")
        wq = []
        wstate = {"next": 0}

        def wload(i):
            src = wq[i]
            slot = i % NW
            srcv = src.rearrange("(k p) c -> p k c", p=128)
            for q in range(4):
                st, stk = wst_ring.get()
                S.op("sp", lambda e, st=st, q=q: e.dma_start(out=st[:], in_=srcv[:, 2 * q:2 * q + 2, :]),
                     writes=[stk], dma="d_" + stk)
                dst = wbuf[slot][:, 2 * q:2 * q + 2, :]
                wk = f"wbuf{slot}_{q}"
                if q == 0:
                    S.op("pool", lambda e, st=st, dst=dst: e.tensor_copy(out=dst, in_=st[:]), reads=[stk], writes=[wk])
                elif q == 2:
                    S.op("dve", lambda e, st=st, dst=dst: e.tensor_copy(out=dst, in_=st[:]), reads=[stk], writes=[wk])
                else:
                    S.op("act", lambda e, st=st, dst=dst: e.activation(out=dst, in_=st[:], func=AF.Copy), reads=[stk],
                         writes=[wk])

        released = set()

        def wneed(i):
            while wstate["next"] < len(wq):
                j = wstate["next"]
                if j - NW >= 0 and (j - NW) not in released:
                    assert j > i, ("weight slot still live", j, i)
                    break
                if j > i + NW - 1:
                    break
                wload(j)
                wstate["next"] += 1
            return i % NW

        def w_done(i):
            released.add(i)

        def colgrp(t, c0):
            return t[:, c0:c0 + 512]

        seq = []
        for j in range(6):
            seq.append(("ada", j, colgrp(w_ada_d, j * 512)))
        P1 = [("k", 0, colgrp(w_in_d, C_K)), ("v", 0, colgrp(w_in_d, C_V)), ("v", 1, colgrp(w_in_d, C_V + 512))]
        P2 = [("q", 0, colgrp(w_in_d, C_Q)), ("k", 0, colgrp(w_in_d, C_K)),
              ("v", 0, colgrp(w_in_d, C_V)), ("v", 1, colgrp(w_in_d, C_V + 512)),
              ("r", 0, colgrp(w_in_d, C_R)), ("r", 1, colgrp(w_in_d, C_R + 512)),
              ("pa", 0, colgrp(wpa_d, 0)), ("ga", 0, colgrp(w_in_d, C_GA)),
              ("pa", 1, colgrp(wpa_d, 512)), ("ga", 1, colgrp(w_in_d, C_GA + 512)),
              ("gv", 0, colgrp(w_in_d, C_GV)), ("gv", 1, colgrp(w_in_d, C_GV + 512)),
              ("u", 0, colgrp(w_in_d, C_U)), ("z", 0, colgrp(w_in_d, C_Z)),
              ("u", 1, colgrp(w_in_d, C_U + 512)), ("z", 1, colgrp(w_in_d, C_Z + 512)),
              ("pb", 0, colgrp(wpb_d, 0)), ("gb", 0, colgrp(w_in_d, C_GB)),
              ("pb", 1, colgrp(wpb_d, 512)), ("gb", 1, colgrp(w_in_d, C_GB + 512)),
              ("wo", 0, colgrp(wo_d, 0)), ("wo", 1, colgrp(wo_d, 512))]
        seq += [("p1",) + g for g in P1]
        for t in range(NTILE):
            seq += [("p2",) + g for g in P2]
        for s_ in seq:
            wq.append(s_[-1])
        wpos = {"i": 0}

        def next_w(expect):
            i = wpos["i"]
            assert seq[i][-3] == expect[0] and seq[i][-2] == expect[1], (seq[i][:-1], expect)
            slot = wneed(i)
            wpos["i"] += 1
            wlive.append(i)
            return slot

        wlive = []

        def w_release_all():
            for i in wlive:
                w_done(i)
            wlive.clear()

        waf = sb("waf", [128, 8, 16])
        S.op("sp", lambda e: e.dma_start(out=waf[:], in_=w_in_d[:, C_A:C_A + 16].rearrange("(k p) c -> p k c", p=128)),
             writes=["waf"], dma="d_wa")
        S.op("dve", lambda e: e.tensor_copy(out=wa[:], in_=waf[:]), reads=["waf"], writes=["wa"])

        def wa2_dma(e):
            return [e.dma_start(out=wa2f[0:16, :], in_=w_a2_d), e.dma_start(out=wa2f[16:17, :], in_=b_a_d)]
        S.op("sp", wa2_dma, writes=["wa2f"], dma="d_c5", ndma=2)
        S.op("dve", lambda e: e.tensor_copy(out=wa2[:], in_=wa2f[:]), reads=["wa2f"], writes=["wa2"])

        stop_here("c2")
        for j in range(6):
            slot = next_w(("ada", j))
            b = PB.alloc()

            def mm(e, slot=slot, b=b):
                r = None
                for fb in range(4):
                    for kc in range(8):
                        r = e.matmul(psum_tiles[b][:, fb * 3:fb * 3 + 3], lhsT=wbuf[slot][:, kc, fb * 128:(fb + 1) * 128],
                                     rhs=scT[:, :, kc], start=(kc == 0), stop=(kc == 7))
                return r
            S.op("pe", mm, reads=[f"wbuf{slot}", "scT"], writes=[pkey(b)])
            for fb in range(4):
                jj = j * 4 + fb
                S.op("dve", lambda e, b=b, fb=fb, jj=jj: e.tensor_scalar(
                    out=adaT[:, jj, :], in0=psum_tiles[b][:, fb * 3:fb * 3 + 3], scalar1=RT[:, 34 + jj:35 + jj],
                    scalar2=None, op0=ALU.add), reads=[pkey(b), "RT"], writes=["adaT"])
            PB.free(b)
            w_release_all()
        for m in range(3):
            S.op("dve", lambda e, m=m: e.tensor_scalar(out=modA[:, m, :], in0=adaT[:, 8:16, m], scalar1=1.0, scalar2=None,
                                                       op0=ALU.add), reads=["adaT"], writes=["modA"])
            S.op("dve", lambda e, m=m: e.tensor_tensor(out=modA[:, m, :], in0=modA[:, m, :], in1=RT[:, 0:8], op=ALU.mult),
                 reads=["modA", "RT"], writes=["modA"])

        stop_here("c3")
        def gate_bc(dst, dkey, n, cols_m):
            for j in range(8):
                for (c0, ncol, m) in cols_m:
                    S.op("dve", lambda e, j=j, c0=c0, ncol=ncol, m=m: e.tensor_scalar(
                        out=GL[:, j, c0:c0 + ncol], in0=onesf[:, c0:c0 + ncol], scalar1=adaT[:, 16 + j, m:m + 1],
                        scalar2=0.5, op0=ALU.mult, op1=ALU.mult), reads=["onesf", "adaT"], writes=[GLk])
            for half in range(2):
                b = PB.alloc()

                def mm(e, b=b, half=half):
                    r = None
                    for jj in range(4):
                        j = half * 4 + jj
                        r = e.matmul(psum_tiles[b][0:n, jj * 128:(jj + 1) * 128], lhsT=GL[:, j, 0:n], rhs=identf[:],
                                     start=True, stop=True)
                    return r
                S.op("pe", mm, reads=[GLk, "identf"], writes=[pkey(b)])
                S.op("act", lambda e, b=b, half=half: e.activation(out=dst[0:n, half * 512:(half + 1) * 512],
                                                                   in_=psum_tiles[b][0:n, :], func=AF.Copy),
                     reads=[pkey(b)], writes=[dkey])
                PB.free(b)
        gate_bc(g05s, "g05s", 32, [(0, 16, 1), (16, 16, 2)])


        stop_here("c4")
        S.op("dve", lambda e: e.memset(e2col[:], 0.0), writes=["e2col"])
        S.op("dve", lambda e: e.memset(e2col[:, 0:1], 1.0), reads=["e2col"], writes=["e2col"])
        S.op("dve", lambda e: e.memset(bs2[:], 0.0), writes=["bs2"])
        S.op("dve", lambda e: e.memset(bs2s[:], 0.0), writes=["bs2s"])
        S.op("dve", lambda e: e.memset(L2, 1.0), writes=[L2k])
        S.op("sp", lambda e: e.dma_start(out=L2t[0:1, :], in_=lnb_d), reads=[L2k], writes=[L2k], dma="d_c9")
        S.op("sp", lambda e: e.dma_start(out=bs2[1:2, :, :], in_=b_s_d.rearrange("(o g) i -> o g i", o=1)),
             reads=["bs2"], writes=["bs2"], dma="d_c10")

        def bs2s_dma(e):
            src = b_s_d[:, 0:16].rearrange("(o g) i -> o g i", o=1)
            return [e.dma_start(out=bs2s[1:2, :, 0:16], in_=src), e.dma_start(out=bs2s[1:2, :, 16:32], in_=src)]
        S.op("sp", bs2s_dma, reads=["bs2s"], writes=["bs2s"], dma="d_c11", ndma=2)

        stop_here("d1")
        for g in range(4):
            S.op("sp", lambda e, g=g: e.dma_start(out=wsf[:], in_=w_s_d[g]), writes=["wsf"], dma="d_c12")
            S.op("dve", lambda e: e.tensor_tensor(out=wsf[:], in0=wsf[:], in1=wsmask[:], op=ALU.mult),
                 reads=["wsf", "wsmask"], writes=["wsf"])
            if g == 0:
                stop_here("e0")
            b = PB.alloc()
            S.op("pe", lambda e, b=b: e.transpose(psum_tiles[b][:, 0:128], wsf[:], identf[:]), reads=["wsf", "identf"],
                 writes=[pkey(b)])
            if g == 0:
                stop_here("e0a")
            S.op("dve", lambda e, b=b, g=g: e.tensor_copy(out=WsTf[:, g, :], in_=psum_tiles[b][:, 0:128]),
                 reads=[pkey(b)], writes=["WsTf"])
            if g == 0:
                stop_here("e0b")
            S.op("act", lambda e, g=g: e.activation(out=WsT[:, g, :], in_=WsTf[:, g, :], func=AF.Copy),
                 reads=["WsTf"], writes=["WsT"])
            PB.free(b)
            if g == 0:
                stop_here("e1")
            S.op("dve", lambda e: e.memset(wsfs[:], 0.0), writes=["wsfs"])

            def wsfs_dma(e, g=g):
                return [e.dma_start(out=wsfs[0:16, 0:16], in_=w_s_d[g, 0:16, 0:16]),
                        e.dma_start(out=wsfs[16:32, 16:32], in_=w_s_d[g, 0:16, 0:16])]
            S.op("sp", wsfs_dma, reads=["wsfs"], writes=["wsfs"], dma="d_c13", ndma=2)
            if g == 0:
                stop_here("e2")
            b = PB.alloc()
            S.op("pe", lambda e, b=b: e.transpose(psum_tiles[b][0:32, 0:32], wsfs[:], identf[0:32, 0:32]),
                 reads=["wsfs", "identf"], writes=[pkey(b)])
            S.op("dve", lambda e, b=b, g=g: e.tensor_copy(out=WsTfs[:, g, :], in_=psum_tiles[b][0:32, 0:32]),
                 reads=[pkey(b)], writes=["WsTfs"])
            S.op("act", lambda e, g=g: e.activation(out=WsTs[:, g, :], in_=WsTfs[:, g, :], func=AF.Copy),
                 reads=["WsTfs"], writes=["WsTs"])
            PB.free(b)
        stop_here("d2")
        for (n, wtf, b2, r2, rb, rbk) in ((128, WsTf, bs2, R2, Rb, "Rb"), (32, WsTfs, bs2s, R2s, Rbs, "Rbs")):
            b = PB.alloc()

            def mm(e, b=b, n=n, wtf=wtf):
                r = None
                for g in range(4):
                    r = e.matmul(psum_tiles[b][0:2, g * n:(g + 1) * n], lhsT=e2col[0:n, :], rhs=wtf[:, g, :],
                                 start=True, stop=True)
                return r
            S.op("pe", mm, reads=["e2col", "WsTf", "WsTfs"], writes=[pkey(b)])
            S.op("dve", lambda e, b=b, n=n, b2=b2, r2=r2: e.tensor_tensor(
                out=r2[:].rearrange("p g i -> p (g i)"), in0=psum_tiles[b][0:2, 0:4 * n],
                in1=b2[:].rearrange("p g i -> p (g i)"), op=ALU.add), reads=[pkey(b), "bs2", "bs2s"], writes=[rbk + "r2"])
            PB.free(b)
            for half in range(2):
                b = PB.alloc()

                def mm2(e, b=b, n=n, half=half, r2=r2):
                    r = None
                    for cc in range(4):
                        c = half * 4 + cc
                        r = e.matmul(psum_tiles[b][:, cc * n:(cc + 1) * n], lhsT=L2t[0:2, c * 128:(c + 1) * 128],
                                     rhs=r2[:, c // 2, :], start=True, stop=True)
                    return r
                S.op("pe", mm2, reads=[L2k, rbk + "r2"], writes=[pkey(b)])
                S.op("dve", lambda e, b=b, n=n, half=half, rb=rb: e.tensor_copy(
                    out=rb[:, half * 4:(half + 1) * 4, :],
                    in_=psum_tiles[b][:, 0:4 * n].rearrange("p (c i) -> p c i", i=n)), reads=[pkey(b)], writes=[rbk])
                PB.free(b)

        stop_here("c5")
        def make_group(name, NT, subs, segs, mods, mask_f, mask_b, wst, rbt, rbk):
            g = Grp()
            g.name, g.NT, g.subs, g.segs, g.mods = name, NT, subs, segs, mods
            g.mask_f, g.mask_b, g.wst, g.rbt, g.rbk = mask_f, mask_b, wst, rbt, rbk
            n = subs[0][1]
            g.n = n
            nsub = len(subs)
            g.hT = sb(f"hT_{name}", [128, 8, NT], BF16)
            g.alrT = sb(f"alrT_{name}", [17, NT], BF16)
            g.E12 = sb(f"E12_{name}", [128, 8, NT])
            g.E1 = g.E12[:, 0:4, :]
            g.E2 = g.E12[:, 4:8, :]
            g.ta = g.E12
            g.qT = sb(f"qT_{name}", [128, 4, NT], BF16)
            g.kT = sb(f"kT_{name}", [128, 4, NT], BF16)
            g.vtok = [sb(f"vx_{name}{i}", [n, D], BF16) for i in range(nsub)]
            g.xhb = g.vtok
            g.on = [sb(f"on_{name}{i}", [n, D], BF16) for i in range(nsub)]
            g.zT = sb(f"zT_{name}", [128, 8, NT], BF16)
            g.uz = sb(f"um_{name}", [128, 8, NT], BF16)
            g.mT = g.uz
            S.op("dve", lambda e: e.memset(g.alrT[:], 1.0), writes=[f"alrT_{name}"])
            return g

        gp = make_group("p", 512, [(i * 128, 128) for i in range(4)], [[(0, 128)]] * 4, [[0]] * 4,
                        masku, maskub, WsT, Rb, "Rb")
        gs = make_group("s", 32, [(0, 32)], [[(0, 16), (16, 16)]], [[1, 2]], maskus, maskusb, WsTs, Rbs, "Rbs")

        Sf_p = sb("Sf_p", [128, 4, 256])
        Sb_p = sb("Sb_p", [128, 4, 256], BF16)
        Dtot = sb("Dtot", [128, 4])
        Sf_s = [sb(f"Sf_s{i}", [128, 4, 256]) for i in range(2)]
        Sb_s = [sb(f"Sb_s{i}", [128, 4, 256], BF16) for i in range(2)]
        S.op("dve", lambda e: e.memset(Sf_p[:], 0.0), writes=["Sf_p"])
        S.op("dve", lambda e: e.memset(Sb_p[:], 0.0), writes=["Sb_p"])
        S.op("dve", lambda e: e.memset(Dtot[:], 1.0), writes=["Dtot"])
        for i in range(2):
            S.op("sp", lambda e, i=i: e.dma_start(out=Sf_s[i][:], in_=s0_d[i].rearrange("h k v -> k h v")),
                 writes=[f"Sf_s{i}"], dma=f"d_s0{i}")
            S.op("act", lambda e, i=i: e.activation(out=Sb_s[i][:].rearrange("p h v -> p (h v)"),
                                                    in_=Sf_s[i][:].rearrange("p h v -> p (h v)"), func=AF.Copy),
                 reads=[f"Sf_s{i}"], writes=[f"Sb_s{i}"])
        gp.Sf, gp.Sb, gp.Skeys = [Sf_p], [Sb_p], ["p"]
        gq = Grp()
        gq.name, gq.NT, gq.subs, gq.segs, gq.mods = "q", 512, gp.subs, gp.segs, gp.mods
        gq.mask_f, gq.mask_b, gq.wst, gq.rbt, gq.rbk, gq.n = masku, maskub, WsT, Rb, "Rb", 128
        gq.hT = gp.zT
        gq.alrT = sb("alrT_q", [17, 512], BF16)
        S.op("dve", lambda e: e.memset(gq.alrT[:], 1.0), writes=["alrT_q"])
        gq.E1 = shareA[:, :].rearrange("p (h t) -> p h t", t=512)
        gq.E2 = gp.uz[:].rearrange("p a b -> p (a b)").bitcast(F32).rearrange("p (h t) -> p h t", t=512)
        gq.kT = gp.qT
        gq.vtok = gp.on
        gq.Sf, gq.Sb, gq.Skeys = gp.Sf, gp.Sb, gp.Skeys
        gs.Sf, gs.Sb, gs.Skeys = Sf_s, Sb_s, ["s0", "s1"]

        out_events = []
        DBG = bool(int(os.environ.get("KDBG", "0")))

        def dbg(name, ap, shape, dt, keys):
            if not DBG:
                return
            d = nc.dram_tensor("dbg_" + name, shape, dt, kind="ExternalOutput").ap()
            out_events.append(S.op("sp", lambda e: e.dma_start(out=d, in_=ap), reads=keys, dma="d_dbg_" + name))

        def rstd_from_ss(st, stk, col0, ncol, n, scale):
            S.op("act", lambda e: e.activation(out=st[0:n, col0:col0 + ncol], in_=st[0:n, col0:col0 + ncol], func=AF.Ln,
                                               scale=scale, bias=EPS), reads=[stk], writes=[stk])
            S.op("act", lambda e: e.activation(out=st[0:n, col0:col0 + ncol], in_=st[0:n, col0:col0 + ncol], func=AF.Exp,
                                               scale=-0.5), reads=[stk], writes=[stk])

        def stage_h(g, xsrc, row0, only=None):
            nm = g.name
            sub_ids = [si for si in range(len(g.subs)) if only is None or si in (only if isinstance(only, (list, tuple)) else [only])]
            for p0 in range(0, len(sub_ids), 2):
                pair = sub_ids[p0:p0 + 2]
                ctx = {}
                pst = st_ring.get()
                for j, si in enumerate(pair):
                    off, n = g.subs[si]
                    xt, xk = x_ring.get()
                    S.op("sp", lambda e, xt=xt, off=off, n=n: e.dma_start(out=xt[0:n, :],
                                                                          in_=xsrc[row0 + off:row0 + off + n, :]),
                         writes=[xk], dma="d_" + xk)
                    xn, xnk = xn_ring.get()
                    ctx[si] = {"xt": xt, "xk": xk, "st": pst, "off": off, "n": n, "xn": xn, "xnk": xnk, "j": j}
                for si in pair:
                    c = ctx[si]
                    xt, xk, (st, stk), n, xn, xnk, j = c["xt"], c["xk"], c["st"], c["n"], c["xn"], c["xnk"], c["j"]
                    S.op("act", lambda e, xt=xt, st=st, n=n, xn=xn, j=j: e.activation(
                        out=xn[0:n, :], in_=xt[0:n, :], func=AF.Square, accum_out=st[0:n, j:j + 1]),
                        reads=[xk], writes=[stk, xnk])
                (st, stk), n, npair = pst, ctx[pair[0]]["n"], len(pair)
                S.op("act", lambda e, st=st, n=n, npair=npair: e.activation(out=st[0:n, 0:npair], in_=st[0:n, 0:npair],
                                                                            func=AF.Ln, scale=1.0 / D, bias=EPS),
                     reads=[stk], writes=[stk])
                S.op("act", lambda e, st=st, n=n, npair=npair: e.activation(out=st[0:n, 0:npair], in_=st[0:n, 0:npair],
                                                                            func=AF.Exp, scale=-0.5),
                     reads=[stk], writes=[stk])
                for si in pair:
                    c = ctx[si]
                    xt, xk, (st, stk), n = c["xt"], c["xk"], c["st"], c["n"]
                    xn, xnk, j = c["xn"], c["xnk"], c["j"]
                    S.op("act", lambda e, xt=xt, st=st, xn=xn, n=n, j=j: e.activation(out=xn[0:n, :], in_=xt[0:n, :],
                                                                                       func=AF.Copy, scale=st[0:n, j:j + 1]),
                         reads=[xk, stk], writes=[xnk])
                for si in pair:
                    c = ctx[si]
                    xn, xnk, n = c["xn"], c["xnk"], c["n"]
                    b = PB.alloc()
                    c["b"] = b
                    pbf = psum_tiles[b][:].bitcast(BF16)
                    c["pbf"] = pbf

                    def tr(e, xn=xn, n=n, pbf=pbf):
                        r = None
                        for kc in range(8):
                            r = e.transpose(pbf[:, kc * 128:kc * 128 + n], xn[0:n, kc * 128:(kc + 1) * 128],
                                            identb[0:n, 0:n])
                        return r
                    S.op("pe", tr, reads=[xnk, "identb"], writes=[pkey(b)])
                for si in pair:
                    c = ctx[si]
                    b, pbf, off = c["b"], c["pbf"], c["off"]
                    on_act = (c["j"] == 1)
                    for kc in range(8):
                        for (s0, sl), m in zip(g.segs[si], g.mods[si]):
                            if on_act:
                                S.op("act", lambda e, kc=kc, s0=s0, sl=sl, m=m, off=off, pbf=pbf: e.activation(
                                    out=g.hT[:, kc, off + s0:off + s0 + sl], in_=pbf[:, kc * 128 + s0:kc * 128 + s0 + sl],
                                    func=AF.Identity, scale=modA[:, m, kc:kc + 1], bias=adaT[:, kc, m:m + 1]),
                                    reads=[pkey(b), "modA", "adaT"], writes=[f"hT_{nm}{si}"])
                            else:
                                S.op("dve", lambda e, kc=kc, s0=s0, sl=sl, m=m, off=off, pbf=pbf: e.tensor_scalar(
                                    out=g.hT[:, kc, off + s0:off + s0 + sl], in0=pbf[:, kc * 128 + s0:kc * 128 + s0 + sl],
                                    scalar1=modA[:, m, kc:kc + 1], scalar2=adaT[:, kc, m:m + 1], op0=ALU.mult, op1=ALU.add),
                                    reads=[pkey(b), "modA", "adaT"], writes=[f"hT_{nm}{si}"])
                    PB.free(b)

        def hT_keys(g):
            return [f"hT_{g.name}{si}" for si in range(len(g.subs))]

        def stage_a(g):
            nm = g.name
            NT = g.NT
            b = PB.alloc()

            def mm(e, b=b):
                r = None
                for kc in range(8):
                    r = e.matmul(psum_tiles[b][0:16, 0:NT], lhsT=wa[:, kc, :], rhs=g.hT[:, kc, :], start=(kc == 0),
                                 stop=(kc == 7))
                return r
            S.op("pe", mm, reads=["wa"] + hT_keys(g), writes=[pkey(b)])
            S.op("dve", lambda e, b=b: e.tensor_copy(out=g.alrT[0:16, :], in_=psum_tiles[b][0:16, 0:NT]),
                 reads=[pkey(b)], writes=[f"alrT_{nm}"])
            PB.free(b)
            nsub = len(g.subs)
            for p0 in range(0, nsub, 2):
                pair = list(range(p0, min(p0 + 2, nsub)))
                ctx = {}
                for si in pair:
                    off, n = g.subs[si]
                    b = PB.alloc()
                    S.op("pe", lambda e, b=b, off=off, n=n: e.matmul(psum_tiles[b][0:n, :], lhsT=g.alrT[:, off:off + n],
                                                                    rhs=wa2[:, :], start=True, stop=True),
                         reads=[f"alrT_{nm}", "wa2"], writes=[pkey(b)])
                    ctx[si] = {"b": b, "off": off, "n": n}
                for si in pair:
                    c = ctx[si]
                    b, n = c["b"], c["n"]
                    et, ek = e_ring.get()
                    c["et"], c["ek"] = et, ek
                    S.op("act", lambda e, b=b, n=n, et=et: e.activation(out=et[0:n, :], in_=psum_tiles[b][0:n, :],
                                                                        func=AF.Exp, scale=-1.0), reads=[pkey(b)], writes=[ek])
                    PB.free(b)
                for si in pair:
                    c = ctx[si]
                    et, ek, n = c["et"], c["ek"], c["n"]
                    spt, spk = sp_ring.get()
                    c["spt"], c["spk"] = spt, spk
                    S.op("act", lambda e, n=n, et=et, spt=spt: e.activation(out=spt[0:n, :], in_=et[0:n, :], func=AF.Ln,
                                                                            bias=1.0), reads=[ek], writes=[spk])
                for si in pair:
                    c = ctx[si]
                    spt, spk, n = c["spt"], c["spk"], c["n"]
                    b = PB.alloc()
                    c["b2"] = b

                    def mmp(e, b=b, n=n, spt=spt):
                        r = None
                        for h in range(4):
                            r = e.matmul(psum_tiles[b][:, h * n:(h + 1) * n], lhsT=spt[0:n, h * 128:(h + 1) * 128],
                                         rhs=g.mask_b[0:n, 0:n], start=True, stop=True)
                        return r
                    S.op("pe", mmp, reads=[spk, "maskub", "maskusb"], writes=[pkey(b)])
                for which, sc in (("E1", -1.0 / 16.0), ("E2", 1.0 / 16.0)):
                    for si in pair:
                        c = ctx[si]
                        b, n, off = c["b2"], c["n"], c["off"]
                        pv = psum_tiles[b][:, 0:4 * n].rearrange("p (h t) -> p h t", t=n)
                        dst = (g.E1 if which == "E1" else g.E2)[:, :, off:off + n]
                        S.op("act", lambda e, pv=pv, dst=dst, sc=sc: e.activation(out=dst, in_=pv, func=AF.Exp, scale=sc),
                             reads=[pkey(b)], writes=[f"{which}_{nm}{si}"] + [f"ta_{nm}{d}" for d in range(8)])
                for si in pair:
                    PB.free(ctx[si]["b2"])

        def Ek(g, which):
            return [f"{which}_{g.name}{si}" for si in range(len(g.subs))]

        def stage_qk(g, slot, is_q):
            nm = g.name
            NT = g.NT
            for h in range(4):
                b = PB.alloc()

                def mm(e, b=b, h=h):
                    r = None
                    for kc in range(8):
                        r = e.matmul(psum_tiles[b][:, 0:NT], lhsT=wbuf[slot][:, kc, h * 128:(h + 1) * 128],
                                     rhs=g.hT[:, kc, :], start=(kc == 0), stop=(kc == 7))
                    return r
                S.op("pe", mm, reads=[f"wbuf{slot}"] + hT_keys(g), writes=[pkey(b)])
                if is_q:
                    S.op("dve", lambda e, b=b, h=h: e.scalar_tensor_tensor(
                        out=g.qT[:, h, :], in0=psum_tiles[b][:, 0:NT], scalar=128.0 ** -0.5, in1=g.E1[:, h, :],
                        op0=ALU.mult, op1=ALU.mult), reads=[pkey(b)] + Ek(g, "E1"), writes=[f"qT_{nm}"])
                else:
                    S.op("dve", lambda e, b=b, h=h: e.tensor_tensor(
                        out=g.kT[:, h, :], in0=psum_tiles[b][:, 0:NT], in1=g.E2[:, h, :], op=ALU.mult),
                        reads=[pkey(b)] + Ek(g, "E2"), writes=[f"kT_{nm}"])
                PB.free(b)

        def tok_major(g, slot, cg, si, evac):
            off, n = g.subs[si]
            b = PB.alloc()

            def mm(e, b=b):
                r = None
                for kc in range(8):
                    r = e.matmul(psum_tiles[b][0:n, :], lhsT=g.hT[:, kc, off:off + n], rhs=wbuf[slot][:, kc, :],
                                 start=(kc == 0), stop=(kc == 7))
                return r
            S.op("pe", mm, reads=[f"wbuf{slot}", f"hT_{g.name}{si}"], writes=[pkey(b)])
            evac(b, n)
            PB.free(b)

        def stage_v(g, slot, cg):
            for si in range(len(g.subs)):
                def ev(b, n, si=si):
                    S.op("act", lambda e: e.activation(out=g.vtok[si][0:n, cg * 512:(cg + 1) * 512],
                                                       in_=psum_tiles[b][0:n, :], func=AF.Copy),
                         reads=[pkey(b)], writes=[f"vx_{g.name}{si}_{cg}"])
                tok_major(g, slot, cg, si, ev)

        def vkeys(g, si):
            return [f"vx_{g.name}{si}_0", f"vx_{g.name}{si}_1"]

        def stage_gla(g, si, full):
            nm = g.name
            off, n = g.subs[si]
            segs = g.segs[si]
            multi = len(segs) > 1
            ktoks = []
            for gi, (s0, sl) in enumerate(segs):
                kh, khk = khT_ring.get()
                if multi:
                    S.op("pool", lambda e, kh=kh: e.memset(kh[:, :, 0:n], 0.0), writes=[khk])
                last = off + s0 + sl - 1
                for h in range(4):
                    S.op("pool", lambda e, kh=kh, h=h, s0=s0, sl=sl, last=last: e.tensor_tensor(
                        out=kh[:, h, s0:s0 + sl], in0=g.kT[:, h, off + s0:off + s0 + sl],
                        in1=g.E1[:, h, last:last + 1].to_broadcast([128, sl]), op=ALU.mult),
                        reads=[f"kT_{nm}", f"E1_{nm}{si}", khk], writes=[khk])
                b = PB.alloc()
                pbf = psum_tiles[b][:].bitcast(BF16)

                def tr(e, kh=kh, pbf=pbf):
                    r = None
                    for h in range(4):
                        r = e.transpose(pbf[0:n, h * 128:(h + 1) * 128], kh[:, h, 0:n], identb[:, :])
                    return r
                S.op("pe", tr, reads=[khk, "identb"], writes=[pkey(b)])
                kt, ktk = ktok_ring.get()
                S.op("act", lambda e, kt=kt, pbf=pbf: e.activation(out=kt[0:n, :], in_=pbf[0:n, 0:512], func=AF.Copy),
                     reads=[pkey(b)], writes=[ktk])
                PB.free(b)
                ktoks.append((kt, ktk))
            if full:
                b = PB.alloc()

                def mma(e, b=b):
                    r = None
                    for h in range(4):
                        r = e.matmul(psum_tiles[b][0:n, h * n:(h + 1) * n], lhsT=g.kT[:, h, off:off + n],
                                     rhs=g.qT[:, h, off:off + n], start=True, stop=True)
                    return r
                S.op("pe", mma, reads=[f"kT_{nm}", f"qT_{nm}"], writes=[pkey(b)])
                at, atk = atm_ring.get()
                for h in range(4):
                    S.op("dve", lambda e, b=b, h=h, at=at: e.tensor_tensor(
                        out=at[0:n, h * n:(h + 1) * n], in0=psum_tiles[b][0:n, h * n:(h + 1) * n], in1=g.mask_f[0:n, 0:n],
                        op=ALU.mult), reads=[pkey(b), "masku", "maskus"], writes=[atk])
                PB.free(b)
                qsegs = []
                for gi, (s0, sl) in enumerate(segs):
                    if multi:
                        qm, qmk = qm_ring.get()
                        S.op("pool", lambda e, qm=qm: e.memset(qm[:, :, 0:n], 0.0), writes=[qmk])
                        S.op("pool", lambda e, qm=qm, s0=s0, sl=sl: e.tensor_copy(
                            out=qm[:, :, s0:s0 + sl], in_=g.qT[:, :, off + s0:off + s0 + sl]),
                            reads=[f"qT_{nm}", qmk], writes=[qmk])
                        qsegs.append((lambda h, qm=qm: qm[:, h, 0:n], qmk))
                    else:
                        qsegs.append((lambda h: g.qT[:, h, off:off + n], f"qT_{nm}"))
                ob = [PB.alloc(), PB.alloc()]

                def mmo(e):
                    r = None
                    for h in range(4):
                        o_ap = psum_tiles[ob[h // 2]][0:n, (h % 2) * 256:(h % 2 + 1) * 256]
                        r = e.matmul(o_ap, lhsT=at[0:n, h * n:(h + 1) * n], rhs=g.vtok[si][0:n, h * 256:(h + 1) * 256],
                                     start=True, stop=False)
                        for gi in range(len(segs)):
                            r = e.matmul(o_ap, lhsT=qsegs[gi][0](h), rhs=g.Sb[gi][:, h, :], start=False,
                                         stop=(gi == len(segs) - 1))
                    return r
                S.op("pe", mmo, reads=[atk] + vkeys(g, si) + [q[1] for q in qsegs] + [f"Sb_{k}" for k in g.Skeys],
                     writes=[pkey(ob[0]), pkey(ob[1])])
                st, stk = st_ring.get()
                for h in range(4):
                    S.op("act", lambda e, h=h, st=st: e.activation(
                        out=g.on[si][0:n, h * 256:(h + 1) * 256],
                        in_=psum_tiles[ob[h // 2]][0:n, (h % 2) * 256:(h % 2 + 1) * 256],
                        func=AF.Square, accum_out=st[0:n, h:h + 1]), reads=[pkey(ob[h // 2])],
                        writes=[stk, f"on_{nm}{si}_{h}"])
                rstd_from_ss(st, stk, 0, 4, n, 1.0 / 256.0)
                for h in range(4):
                    S.op("dve", lambda e, h=h, st=st: e.tensor_scalar(
                        out=g.on[si][0:n, h * 256:(h + 1) * 256],
                        in0=psum_tiles[ob[h // 2]][0:n, (h % 2) * 256:(h % 2 + 1) * 256], scalar1=st[0:n, h:h + 1],
                        scalar2=None, op0=ALU.mult), reads=[pkey(ob[h // 2]), stk], writes=[f"on_{nm}{si}_{h}"])
                PB.free(ob[0])
                PB.free(ob[1])
            for gi, (s0, sl) in enumerate(segs):
                kt, ktk = ktoks[gi]
                last = off + s0 + sl - 1
                ub = [PB.alloc(), PB.alloc()]

                def mmu(e, kt=kt, ub=ub):
                    r = None
                    for h in range(4):
                        r = e.matmul(psum_tiles[ub[h // 2]][:, (h % 2) * 256:(h % 2 + 1) * 256],
                                     lhsT=kt[0:n, h * 128:(h + 1) * 128], rhs=g.vtok[si][0:n, h * 256:(h + 1) * 256],
                                     start=True, stop=True)
                    return r
                S.op("pe", mmu, reads=[ktk] + vkeys(g, si), writes=[pkey(ub[0]), pkey(ub[1])])
                sk = g.Skeys[gi]
                for h in range(4):
                    S.op("dve", lambda e, h=h, gi=gi, last=last, ub=ub: e.scalar_tensor_tensor(
                        out=g.Sf[gi][:, h, :], in0=g.Sf[gi][:, h, :], scalar=g.E1[:, h, last:last + 1],
                        in1=psum_tiles[ub[h // 2]][:, (h % 2) * 256:(h % 2 + 1) * 256], op0=ALU.mult, op1=ALU.add),
                        reads=[pkey(ub[h // 2]), f"E1_{nm}{si}", f"Sf_{sk}"], writes=[f"Sf_{sk}"])
                PB.free(ub[0])
                PB.free(ub[1])
                if full:
                    S.op("act", lambda e, gi=gi: e.activation(out=g.Sb[gi][:].rearrange("p h v -> p (h v)"),
                                                              in_=g.Sf[gi][:].rearrange("p h v -> p (h v)"), func=AF.Copy),
                         reads=[f"Sf_{sk}"], writes=[f"Sb_{sk}"])

        def stage_r(g, slots):
            nm = g.name
            for si in range(len(g.subs)):
                off, n = g.subs[si]
                sr, srk = sr_ring.get()
                for cg in range(2):
                    def ev(b, n, sr=sr, srk=srk, cg=cg):
                        S.op("act", lambda e: e.activation(out=sr[0:n, cg * 512:(cg + 1) * 512], in_=psum_tiles[b][0:n, :],
                                                           func=AF.Silu), reads=[pkey(b)], writes=[srk])
                    tok_major(g, slots[cg], cg, si, ev)
                za, zak = za_ring.get()
                S.op("dve", lambda e, za=za, sr=sr, n=n, si=si: e.tensor_tensor(
                    out=za[0:n, :], in0=g.on[si][0:n, :], in1=sr[0:n, :], op=ALU.mult),
                    reads=[f"on_{nm}{si}_{h}" for h in range(4)] + [srk], writes=[zak])
                b = PB.alloc()
                pbf = psum_tiles[b][:].bitcast(BF16)

                def tr(e, za=za, n=n, pbf=pbf):
                    r = None
                    for c in range(8):
                        r = e.transpose(pbf[:, c * 128:c * 128 + n], za[0:n, c * 128:(c + 1) * 128], identb[0:n, 0:n])
                    return r
                S.op("pe", tr, reads=[zak, "identb"], writes=[pkey(b)])
                pv = pbf.rearrange("p (h f t) -> p h f t", f=2, t=128)
                zv = g.zT[:].rearrange("p (h f) t -> p h f t", f=2)
                for half in range(2):
                    S.op("dve", lambda e, half=half, pv=pv, zv=zv, off=off, n=n: e.tensor_scalar(
                        out=zv[:, :, half, off:off + n], in0=pv[:, :, half, 0:n], scalar1=RT[:, 8 + half:9 + half],
                        scalar2=None, op0=ALU.mult), reads=[pkey(b), "RT"], writes=[f"zT_{nm}{si}"])
                PB.free(b)

        def zT_keys(g):
            return [f"zT_{g.name}{si}" for si in range(len(g.subs))]

        def stage_proj(g, slot_p, slot_g, cg, branch):
            nm = g.name
            NT = g.NT
            for dd in range(4):
                d = cg * 4 + dd
                by = PB.alloc()

                def mmy(e, by=by, dd=dd):
                    r = None
                    for c in range(8):
                        r = e.matmul(psum_tiles[by][:, 0:NT], lhsT=wbuf[slot_p][:, c, dd * 128:(dd + 1) * 128],
                                     rhs=g.zT[:, c, :], start=(c == 0), stop=(c == 7))
                    return r
                S.op("pe", mmy, reads=[f"wbuf{slot_p}"] + zT_keys(g), writes=[pkey(by)])
                bg = PB.alloc()

                def mmg(e, bg=bg, dd=dd):
                    r = None
                    for kc in range(8):
                        r = e.matmul(psum_tiles[bg][:, 0:NT], lhsT=wbuf[slot_g][:, kc, dd * 128:(dd + 1) * 128],
                                     rhs=g.hT[:, kc, :], start=(kc == 0), stop=(kc == 7))
                    return r
                S.op("pe", mmg, reads=[f"wbuf{slot_g}"] + hT_keys(g), writes=[pkey(bg)])
                tg, tgk = tg_ring.get()
                S.op("act", lambda e, bg=bg, tg=tg, d=d: e.activation(out=tg[:, 0:NT], in_=psum_tiles[bg][:, 0:NT],
                                                                      func=AF.Tanh, scale=0.5, bias=hbg[:, branch, d:d + 1]),
                     reads=[pkey(bg), "hbg"], writes=[tgk])
                PB.free(bg)
                if branch == 0:
                    S.op("dve", lambda e, by=by, tg=tg, d=d: e.scalar_tensor_tensor(
                        out=g.ta[:, d, :], in0=tg[:, 0:NT], scalar=1.0, in1=psum_tiles[by][:, 0:NT], op0=ALU.add,
                        op1=ALU.mult), reads=[pkey(by), tgk], writes=[f"ta_{nm}{d}"] + Ek(g, "E1") + Ek(g, "E2"))
                else:
                    tm, tmk = tm_ring.get()
                    S.op("dve", lambda e, by=by, tg=tg, tm=tm: e.scalar_tensor_tensor(
                        out=tm[:, 0:NT], in0=tg[:, 0:NT], scalar=1.0, in1=psum_tiles[by][:, 0:NT], op0=ALU.add,
                        op1=ALU.mult), reads=[pkey(by), tgk], writes=[tmk])
                    S.op("dve", lambda e, tm=tm, d=d: e.tensor_tensor(out=g.mT[:, d, :], in0=tm[:, 0:NT], in1=g.ta[:, d, :],
                                                                      op=ALU.add),
                         reads=[tmk, f"ta_{nm}{d}"], writes=[f"um_{nm}{d}"])
                PB.free(by)

        def stage_gv(g, slots):
            nm = g.name
            for si in range(len(g.subs)):
                off, n = g.subs[si]
                st, stk = st_ring.get()
                banks = []
                for cg in range(2):
                    b = PB.alloc()

                    def mm(e, b=b, off=off, n=n, cg=cg):
                        r = None
                        for kc in range(8):
                            r = e.matmul(psum_tiles[b][0:n, :], lhsT=g.hT[:, kc, off:off + n], rhs=wbuf[slots[cg]][:, kc, :],
                                         start=(kc == 0), stop=(kc == 7))
                        return r
                    S.op("pe", mm, reads=[f"wbuf{slots[cg]}", f"hT_{nm}{si}"], writes=[pkey(b)])
                    banks.append(b)
                    S.op("act", lambda e, b=b, n=n, st=st, cg=cg: e.activation(
                        out=g.xhb[si][0:n, cg * 512:(cg + 1) * 512], in_=psum_tiles[b][0:n, :], func=AF.Identity,
                        accum_out=st[0:n, cg:cg + 1]), reads=[pkey(b)], writes=[stk, f"vx_{nm}{si}_{cg}"])
                    S.op("act", lambda e, b=b, n=n, st=st, cg=cg: e.activation(
                        out=g.xhb[si][0:n, cg * 512:(cg + 1) * 512], in_=psum_tiles[b][0:n, :], func=AF.Square,
                        accum_out=st[0:n, 2 + cg:3 + cg]), reads=[pkey(b)], writes=[stk, f"vx_{nm}{si}_{cg}"])
                S.op("dve", lambda e, st=st, n=n: e.tensor_tensor(out=st[0:n, 4:5], in0=st[0:n, 0:1], in1=st[0:n, 1:2],
                                                                  op=ALU.add), reads=[stk], writes=[stk])
                S.op("dve", lambda e, st=st, n=n: e.tensor_tensor(out=st[0:n, 5:6], in0=st[0:n, 2:3], in1=st[0:n, 3:4],
                                                                  op=ALU.add), reads=[stk], writes=[stk])
                S.op("dve", lambda e, st=st, n=n: e.tensor_scalar(out=st[0:n, 4:6], in0=st[0:n, 4:6], scalar1=1.0 / D,
                                                                  scalar2=None, op0=ALU.mult), reads=[stk], writes=[stk])
                S.op("dve", lambda e, st=st, n=n: e.tensor_tensor(out=st[0:n, 6:7], in0=st[0:n, 4:5], in1=st[0:n, 4:5],
                                                                  op=ALU.mult), reads=[stk], writes=[stk])
                S.op("dve", lambda e, st=st, n=n: e.tensor_tensor(out=st[0:n, 5:6], in0=st[0:n, 5:6], in1=st[0:n, 6:7],
                                                                  op=ALU.subtract), reads=[stk], writes=[stk])
                rstd_from_ss(st, stk, 5, 1, n, 1.0)
                S.op("dve", lambda e, st=st, n=n: e.scalar_tensor_tensor(
                    out=st[0:n, 6:7], in0=st[0:n, 4:5], scalar=-1.0, in1=st[0:n, 5:6], op0=ALU.mult, op1=ALU.mult),
                    reads=[stk], writes=[stk])
                if nm == "s":
                    vnf, vnk = xo_ring.get()
                    lwt, lwk = x_ring.get()
                    lbt, lbk = x_ring.get()
                    S.op("sp", lambda e, lwt=lwt, n=n: e.dma_start(out=lwt[0:n, :], in_=bass.AP(lnw_h, 0, [[0, n], [1, D]])),
                         writes=[lwk], dma="d_" + lwk)
                    S.op("sp", lambda e, lbt=lbt, n=n: e.dma_start(out=lbt[0:n, :], in_=bass.AP(lnb_h, 0, [[0, n], [1, D]])),
                         writes=[lbk], dma="d_" + lbk)
                for c2, bb in enumerate(banks):
                    S.op("act", lambda e, c2=c2, bb=bb, n=n, st=st, si=si: e.activation(
                        out=g.xhb[si][0:n, c2 * 512:(c2 + 1) * 512], in_=psum_tiles[bb][0:n, :], func=AF.Identity,
                        scale=st[0:n, 5:6], bias=st[0:n, 6:7]), reads=[pkey(bb), stk], writes=[f"vx_{nm}{si}_{c2}"])
                    if nm == "s":
                        S.op("act", lambda e, c2=c2, bb=bb, n=n, st=st, vnf=vnf: e.activation(
                            out=vnf[0:n, c2 * 512:(c2 + 1) * 512], in_=psum_tiles[bb][0:n, :], func=AF.Identity,
                            scale=st[0:n, 5:6], bias=st[0:n, 6:7]), reads=[pkey(bb), stk], writes=[vnk])
                    PB.free(bb)
                if nm == "s":
                    S.op("dve", lambda e, n=n, vnf=vnf, lwt=lwt: e.tensor_tensor(out=vnf[0:n, :], in0=vnf[0:n, :],
                                                                                 in1=lwt[0:n, :], op=ALU.mult),
                         reads=[vnk, lwk], writes=[vnk])
                    S.op("dve", lambda e, n=n, vnf=vnf, lbt=lbt: e.tensor_tensor(out=vnf[0:n, :], in0=vnf[0:n, :],
                                                                                 in1=lbt[0:n, :], op=ALU.add),
                         reads=[vnk, lbk], writes=[vnk])
                    out_events.append(S.op("sp", lambda e, n=n, vnf=vnf: e.dma_start(out=vn_d[0:n, :], in_=vnf[0:n, :]),
                                           reads=[vnk], dma="d_vn"))

        def stage_uz(g, slot_u, slot_z, cg):
            nm = g.name
            NT = g.NT
            for cc in range(4):
                c = cg * 4 + cc
                bu = PB.alloc()
                bz = PB.alloc()

                def mm(e, bank, slot, cc=cc):
                    r = None
                    for kc in range(8):
                        r = e.matmul(psum_tiles[bank][:, 0:NT], lhsT=wbuf[slot][:, kc, cc * 128:(cc + 1) * 128],
                                     rhs=g.hT[:, kc, :], start=(kc == 0), stop=(kc == 7))
                    return r
                S.op("pe", lambda e, bu=bu, mm=mm: mm(e, bu, slot_u), reads=[f"wbuf{slot_u}"] + hT_keys(g), writes=[pkey(bu)])
                S.op("pe", lambda e, bz=bz, mm=mm: mm(e, bz, slot_z), reads=[f"wbuf{slot_z}"] + hT_keys(g), writes=[pkey(bz)])
                sz, szk = sz_ring.get()
                S.op("act", lambda e, bz=bz, sz=sz: e.activation(out=sz[:, 0:NT], in_=psum_tiles[bz][:, 0:NT], func=AF.Silu),
                     reads=[pkey(bz)], writes=[szk])
                PB.free(bz)
                S.op("dve", lambda e, bu=bu, sz=sz, c=c: e.tensor_tensor(out=g.uz[:, c, :], in0=psum_tiles[bu][:, 0:NT],
                                                                         in1=sz[:, 0:NT], op=ALU.mult),
                     reads=[pkey(bu), szk], writes=[f"um_{nm}{c}"])
                PB.free(bu)

        def stage_spatial(g):
            nm = g.name
            for si in range(len(g.subs)):
                off, n = g.subs[si]
                for half in range(2):
                    b = PB.alloc()

                    def mm(e, b=b, half=half, si=si, n=n):
                        r = None
                        for cc in range(4):
                            c = half * 4 + cc
                            r = e.matmul(psum_tiles[b][:, cc * n:(cc + 1) * n], lhsT=g.xhb[si][0:n, c * 128:(c + 1) * 128],
                                         rhs=g.wst[0:n, c // 2, 0:n], start=True, stop=True)
                        return r
                    S.op("pe", mm, reads=[f"vx_{nm}{si}_0", f"vx_{nm}{si}_1", "WsT", "WsTs"], writes=[pkey(b)])
                    s1, s1k = s1_ring.get()
                    for cc in range(4):
                        c = half * 4 + cc
                        S.op("dve", lambda e, b=b, cc=cc, c=c, s1=s1, n=n: e.scalar_tensor_tensor(
                            out=s1[:, cc * n:(cc + 1) * n], in0=psum_tiles[b][:, cc * n:(cc + 1) * n],
                            scalar=RT[:, 10 + c:11 + c], in1=g.rbt[:, c, 0:n], op0=ALU.mult, op1=ALU.add),
                            reads=[pkey(b), "RT", g.rbk], writes=[s1k])
                    PB.free(b)
                    S.op("dve", lambda e, s1=s1, half=half, off=off, n=n: e.tensor_tensor(
                        out=g.zT[:, half * 4:(half + 1) * 4, off:off + n],
                        in0=s1[:, 0:4 * n].rearrange("p (c i) -> p c i", i=n),
                        in1=g.uz[:, half * 4:(half + 1) * 4, off:off + n], op=ALU.mult),
                        reads=[s1k] + [f"um_{nm}{half * 4 + cc}" for cc in range(4)], writes=[f"zT_{nm}{si}"])

        def stage_out(g, slots, xsrc, row0, ydst):
            nm = g.name
            g05, g05k = (g05p, "g05p") if nm == "p" else (g05s, "g05s")
            for si in range(len(g.subs)):
                off, n = g.subs[si]
                xr, xrk = xr_ring.get()
                S.op("sp", lambda e, xr=xr, off=off, n=n: e.dma_start(out=xr[0:n, :], in_=xsrc[row0 + off:row0 + off + n, :]),
                     writes=[xrk], dma="d_" + xrk)
                xo, xok = xo_ring.get()
                for cg in range(2):
                    b = PB.alloc()

                    def mm(e, b=b, off=off, n=n, cg=cg):
                        r = None
                        for dch in range(8):
                            r = e.matmul(psum_tiles[b][0:n, :], lhsT=g.mT[:, dch, off:off + n],
                                         rhs=wbuf[slots[cg]][:, dch, :], start=(dch == 0), stop=(dch == 7))
                        return r
                    S.op("pe", mm, reads=[f"wbuf{slots[cg]}"] + [f"um_{nm}{d}" for d in range(8)], writes=[pkey(b)])
                    S.op("dve", lambda e, b=b, n=n, xo=xo, cg=cg: e.tensor_tensor(
                        out=xo[0:n, cg * 512:(cg + 1) * 512], in0=psum_tiles[b][0:n, :],
                        in1=g05[0:n, cg * 512:(cg + 1) * 512], op=ALU.mult), reads=[pkey(b), g05k], writes=[xok])
                    PB.free(b)
                S.op("dve", lambda e, n=n, xo=xo, xr=xr: e.tensor_tensor(out=xo[0:n, :], in0=xo[0:n, :], in1=xr[0:n, :],
                                                                         op=ALU.add),
                     reads=[xok, xrk], writes=[xok])
                st, stk = st_ring.get()
                scr, scrk = xn_ring.get()
                S.op("act", lambda e, n=n, xo=xo, st=st, scr=scr: e.activation(out=scr[0:n, :], in_=xo[0:n, :],
                                                                               func=AF.Square, accum_out=st[0:n, 0:1]),
                     reads=[xok], writes=[stk, scrk])
                rstd_from_ss(st, stk, 0, 1, n, 1.0 / D)
                S.op("dve", lambda e, n=n, xo=xo, xr=xr, st=st: e.scalar_tensor_tensor(
                    out=xr[0:n, :], in0=xo[0:n, :], scalar=st[0:n, 0:1], in1=fnw_bc[0:n, :], op0=ALU.mult, op1=ALU.mult),
                    reads=[xok, stk, "fnw_bc", xrk], writes=[xrk])
                out_events.append(S.op("sp", lambda e, n=n, xr=xr, off=off: e.dma_start(
                    out=ydst[row0 + off:row0 + off + n, :], in_=xr[0:n, :]), reads=[xrk], dma="d_y" + xrk))

        class _Stop(Exception):
            pass

        def finish():
            evs = list(out_events) + [(k, S.count[k]) for k in S.semkeys]
            S.final_waits("sp", evs)

        try:
          if KSTOP == "prologue":
            raise _Stop()
          w_release_all()
          slot_k = next_w(("k", 0))
          slots_v = [next_w(("v", 0)), next_w(("v", 1))]
          def p1_tail(g, t):
              S.op("dve", lambda e, t=t: e.tensor_scalar(out=Sf_p[:].rearrange("p h v -> p (h v)"),
                                                         in0=Sf_p[:].rearrange("p h v -> p (h v)"),
                                                         scalar1=pmaskt[:, t:t + 1], scalar2=None, op0=ALU.mult),
                   reads=["Sf_p", "pmaskt"], writes=["Sf_p"])

          p1g = [gp, gq]
          for t in range(NPREV + 1):
              g = p1g[t % 2]
              gprev = p1g[(t - 1) % 2]
              cur = t < NPREV
              prev = t >= 1
              if cur:
                  stage_h(g, xprev_d, t * 512)
              if prev:
                  stage_gla(gprev, 0, False)
              if cur:
                  stage_a(g)
              if prev:
                  stage_gla(gprev, 1, False)
              if cur:
                  stage_v(g, slots_v[0], 0)
              if prev:
                  stage_gla(gprev, 2, False)
              if cur:
                  stage_v(g, slots_v[1], 1)
              if prev:
                  stage_gla(gprev, 3, False)
                  p1_tail(gprev, t - 1)
              if cur:
                  stage_qk(g, slot_k, False)
          w_release_all()
          S.op("act", lambda e: e.activation(out=Sb_p[:].rearrange("p h v -> p (h v)"),
                                             in_=Sf_p[:].rearrange("p h v -> p (h v)"), func=AF.Copy),
               reads=["Sf_p"], writes=["Sb_p"])
          S.barrier()
          gate_bc(g05p, "g05p", 128, [(0, 128, 0)])
          S.op("sp", lambda e: e.dma_start(out=fnw_bc, in_=bass.AP(fnw_h, 0, [[0, 128], [1, D]])),
               writes=["fnw_bc"], dma="d_c6")
          stop_here("phase1")
          stop_here("exchange")
          for t in range(NTILE):
              groups = [(gp, x_d, t * 512, y_d)]
              if t == 0:
                  groups.append((gs, xs_d, 0, ys_d))
              for (g, xsrc, row0, ydst) in groups:
                  if g is gp and t > 0:
                      continue
                  stage_h(g, xsrc, row0)
                  stage_a(g)
              if t == 0:
                  stop_here("s_a")
              slot = next_w(("q", 0))
              for (g, *_r) in groups:
                  stage_qk(g, slot, True)
              w_release_all()
              if t == 0:
                  stop_here("s_q")
              slot = next_w(("k", 0))
              for (g, *_r) in groups:
                  stage_qk(g, slot, False)
              w_release_all()
              if t == 0:
                  stop_here("s_k")
              for cg in range(2):
                  slot = next_w(("v", cg))
                  for (g, *_r) in groups:
                      stage_v(g, slot, cg)
                  w_release_all()
              if t == 0:
                  stop_here("s_v")
              for (g, *_r) in groups:
                  for si in range(len(g.subs)):
                      stage_gla(g, si, True)
              if t == 0:
                  stop_here("s_gla")
              slots = [next_w(("r", 0)), next_w(("r", 1))]
              for (g, *_r) in groups:
                  stage_r(g, slots)
              w_release_all()
              if t == 0:
                  stop_here("s_r")
              for cg in range(2):
                  slot_p = next_w(("pa", cg))
                  slot_g = next_w(("ga", cg))
                  for (g, *_r) in groups:
                      stage_proj(g, slot_p, slot_g, cg, 0)
                  w_release_all()
              if t == 0:
                  stop_here("s_pa")
              slots = [next_w(("gv", 0)), next_w(("gv", 1))]
              for (g, *_r) in groups:
                  stage_gv(g, slots)
              w_release_all()
              if t == 0:
                  stop_here("s_gv")
              for cg in range(2):
                  slot_u = next_w(("u", cg))
                  slot_z = next_w(("z", cg))
                  for (g, *_r) in groups:
                      stage_uz(g, slot_u, slot_z, cg)
                  w_release_all()
              if t == 0:
                  stop_here("s_uz")
              for (g, *_r) in groups:
                  stage_spatial(g)
              if t == 0:
                  stop_here("s_sp")
              for cg in range(2):
                  slot_p = next_w(("pb", cg))
                  slot_g = next_w(("gb", cg))
                  for (g, *_r) in groups:
                      stage_proj(g, slot_p, slot_g, cg, 1)
                  w_release_all()
              if t == 0:
                  stop_here("s_pb")
              slots = [next_w(("wo", 0)), next_w(("wo", 1))]
              if t + 1 < NTILE:
                  stage_h(gp, x_d, (t + 1) * 512)
                  stage_a(gp)
              for (g, xsrc, row0, ydst) in groups:
                  stage_out(g, slots, xsrc, row0, ydst)
              w_release_all()
              if t == 0:
                  dbg("hT", gp.hT[:], [128, 8, 512], BF16, hT_keys(gp))
                  dbg("qT", gp.qT[:], [128, 4, 512], BF16, ["qT_p"])
                  dbg("kT", gp.kT[:], [128, 4, 512], BF16, ["kT_p"])
                  dbg("on0", gp.on[0][:], [128, D], BF16, ["on_p0"])
                  dbg("on1", gp.on[1][:], [128, D], BF16, ["on_p1"])
                  dbg("ta", gp.ta[:], [128, 8, 512], F32, [f"ta_p{d}" for d in range(8)])
                  dbg("zbT", gp.zT[:], [128, 8, 512], BF16, zT_keys(gp))
                  dbg("mT", gp.mT[:], [128, 8, 512], BF16, [f"um_p{d}" for d in range(8)])
                  dbg("xh0", gp.xhb[0][:], [128, D], BF16, ["vx_p0_0", "vx_p0_1"])
                  dbg("g05p", g05p[:], [128, D], F32, ["g05p"])
                  for i in range(2):
                      out_events.append(S.op("sp", lambda e, i=i: e.dma_start(out=ss_d[i], in_=Sf_s[i][:]),
                                             reads=[f"Sf_s{i}"], dma="d_sso"))
          out_events.append(S.op("sp", lambda e: e.dma_start(out=sp_d, in_=Sf_p[:]), reads=["Sf_p"], dma="d_spo"))
          finish()
        except _Stop:
          finish()

        sems = {k: es.enter_context(nc.semaphore(k)) for k in S.semkeys}
        with nc.Block() as block:
            @block.tensor
            def _(e):
                S.run("pe", e, sems)

            @block.scalar
            def _(e):
                S.run("act", e, sems)

            @block.vector
            def _(e):
                S.run("dve", e, sems)

            @block.gpsimd
            def _(e):
                S.run("pool", e, sems)

            @block.sync
            def _(e):
                S.run("sp", e, sems)
    return nc


_CACHE = {}


def _consts():
    ident = np.eye(128, dtype=np.float32)
    s = np.arange(128)
    masku = (s[:, None] <= s[None, :]).astype(np.float32)
    s32 = np.arange(32)
    maskus = ((s32[:, None] <= s32[None, :]) & ((s32[:, None] // 16) == (s32[None, :] // 16))).astype(np.float32)
    idx = s // 64
    wsmask = (idx[:, None] >= idx[None, :]).astype(np.float32)
    return ident, masku, maskus, wsmask


def kernel(x_prompt, x_sample, state_gla, c_prompt, c_sample, norm_w, w_ada, b_ada, w_in, w_a2, b_a,
           gla_norm_w, ln_v_w, ln_v_b, w_s, b_s, b_gate, w_proj_a, w_proj_b, w_out, final_norm_w):
    f = lambda a: np.ascontiguousarray(np.asarray(a, dtype=np.float32))
    x_prompt, x_sample, state_gla = f(x_prompt), f(x_sample), f(state_gla)
    c_prompt, c_sample = f(c_prompt), f(c_sample)
    if "nc" not in _CACHE:
        _CACHE["nc"] = build_nc()
    nc = _CACHE["nc"]
    ident, masku, maskus, wsmask = _consts()
    shared = {
        "norm_w": f(norm_w).reshape(1, D), "w_ada": f(w_ada)[0], "b_ada": f(b_ada).reshape(1, 3 * D),
        "w_in": f(w_in)[0], "w_a2": f(w_a2)[0], "b_a": f(b_a).reshape(1, 512), "gnw": f(gla_norm_w).reshape(1, 256),
        "lnw": f(ln_v_w).reshape(1, D), "lnb": f(ln_v_b).reshape(1, D), "w_s": f(w_s)[0], "b_s": f(b_s)[0],
        "b_gate": f(b_gate)[0], "wpa": f(w_proj_a)[0], "wpb": f(w_proj_b)[0], "wo": f(w_out)[0],
        "fnw": f(final_norm_w).reshape(1, D), "ident": ident, "masku": masku, "maskus": maskus, "wsmask": wsmask,
    }
    in_maps = []
    for c in range(NCORES):
        m = dict(shared)
        m["x"] = x_prompt[0, c * TOK:(c + 1) * TOK, :]
        m["xs"] = x_sample[2 * c:2 * c + 2].reshape(32, D)
        m["s0"] = state_gla[0, 2 * c:2 * c + 2]
        m["c3"] = np.concatenate([c_prompt[0:1], c_sample[2 * c:2 * c + 2]], axis=0)
        npv = 28 * 512
        xp = np.zeros((npv, D), np.float32)
        if c > 0:
            xp[npv - c * TOK:] = x_prompt[0, :c * TOK, :]
        m["xprev"] = xp
        pm = np.zeros((128, 28), np.float32)
        pm[:, 28 - 4 * c:] = 1.0
        m["pmask"] = pm
        in_maps.append(m)
    res = run_bass_kernel_spmd(nc, in_maps, core_ids=list(range(NCORES)))
    R = res.results
    y_prompt = np.concatenate([R[c]["y"] for c in range(NCORES)], axis=0)[None]
    y_sample = np.concatenate([R[c]["ys"].reshape(2, 16, D) for c in range(NCORES)], axis=0)
    sp = np.transpose(R[NCORES - 1]["sp_out"], (1, 0, 2))[None, None]
    ss = np.concatenate([np.transpose(R[c]["ss_out"], (0, 2, 1, 3)) for c in range(NCORES)], axis=0)[None]
    vn = np.concatenate([R[c]["vn_out"].reshape(2, 16, D) for c in range(NCORES)], axis=0)[None]
    return (y_prompt.astype(np.float32), y_sample.astype(np.float32), np.ascontiguousarray(sp, dtype=np.float32),
            np.ascontiguousarray(ss, dtype=np.float32), vn.astype(np.float32))
```
